# Optimizing a Trainium2 kernel written in Bass

```python
import jax, jax.numpy as jnp
from jax import lax
import numpy as np

D_MODEL = 2048
BATCH = 4
SEQ = 8192
DEPTH = 4
DEC_BATCH = 4
DEC_SEQ = 2048
PAST_LEN = 128

HEAD_DIM = 128
BLOCK = 128
GRID_W = 64
EPS = 1e-6
A_WIDTH = D_MODEL // 2
A_GROUPS = A_WIDTH // HEAD_DIM
A_CHUNK = 128
B_WIDTH = D_MODEL // 2
B_HEADS = B_WIDTH // HEAD_DIM
B_KV_HEADS = 2
B_WINDOW = 128
ROPE_THETA = 500000.0
ROPE_DIMS = HEAD_DIM // 4
C_WIDTH = D_MODEL
C_HEADS = C_WIDTH // HEAD_DIM
C_KV_HEADS = 4
AXIAL_THETA = 10000.0
N_EVEN = (DEPTH + 1) // 2
N_ODD = DEPTH // 2
AB_IN = 3 * A_WIDTH + 2 * B_WIDTH + 2 * B_KV_HEADS * HEAD_DIM
C_IN = 2 * C_WIDTH + 2 * C_KV_HEADS * HEAD_DIM

kernel_name = "hybrid_gmlp_window_axial_encoder"


def _split_points(sizes):
    pts, acc = [], 0
    for s in sizes[:-1]:
        acc += s
        pts.append(acc)
    return pts


def rms_norm(x, g):
    xf = x.astype(jnp.float32)
    y = xf * lax.rsqrt(jnp.mean(xf * xf, axis=-1, keepdims=True) + EPS)
    return (y * g.astype(jnp.float32)).astype(x.dtype)


def rope(x, pos, theta):
    d = x.shape[-1]
    inv = jnp.power(jnp.float32(theta), -jnp.arange(d // 2, dtype=jnp.float32) * (2.0 / d))
    ang = pos[:, None] * inv[None, :]
    cos = jnp.cos(ang)[:, None, :]
    sin = jnp.sin(ang)[:, None, :]
    xf = x.astype(jnp.float32)
    x1, x2 = xf[..., : d // 2], xf[..., d // 2:]
    out = jnp.concatenate([x1 * cos - x2 * sin, x2 * cos + x1 * sin], axis=-1)
    return out.astype(x.dtype)


def partial_rope(x, pos):
    return jnp.concatenate([rope(x[..., :ROPE_DIMS], pos, ROPE_THETA), x[..., ROPE_DIMS:]], axis=-1)


def axial_rope(x, row, col):
    half = x.shape[-1] // 2
    return jnp.concatenate([rope(x[..., :half], row, AXIAL_THETA), rope(x[..., half:], col, AXIAL_THETA)], axis=-1)


def mixer_a(u, v, v_norm, w_s, b_s):
    bn, s, _ = v.shape
    nc = s // A_CHUNK
    vn = rms_norm(v.reshape(bn, s, A_GROUPS, HEAD_DIM), v_norm.reshape(A_GROUPS, HEAD_DIM))
    vc = vn.reshape(bn, nc, A_CHUNK, A_GROUPS, HEAD_DIM)
    mixed = jnp.einsum('gpq,bcqgd->bcpgd', w_s, vc) + b_s.T[None, None, :, :, None]
    return u * mixed.reshape(bn, s, A_WIDTH)


def mixer_b(q, k, v, q_norm, k_norm, sink, pos):
    bn, s, _ = q.shape
    nb = s // BLOCK
    rep = B_HEADS // B_KV_HEADS
    q = partial_rope(rms_norm(q.reshape(bn, s, B_HEADS, HEAD_DIM), q_norm), pos)
    k = partial_rope(rms_norm(k.reshape(bn, s, B_KV_HEADS, HEAD_DIM), k_norm), pos)
    v = v.reshape(bn, s, B_KV_HEADS, HEAD_DIM)
    qb = q.reshape(bn, nb, BLOCK, B_KV_HEADS, rep, HEAD_DIM)
    pad = ((0, 0), (BLOCK, BLOCK), (0, 0), (0, 0))
    kp = jnp.pad(k, pad)
    vp = jnp.pad(v, pad)
    idx = jnp.arange(nb)[:, None] * BLOCK + jnp.arange(3 * BLOCK)[None, :]
    kb = kp[:, idx]
    vb = vp[:, idx]
    scale = HEAD_DIM ** -0.5
    sc = jnp.einsum('bnqhrd,bnkhd->bnhrqk', qb, kb).astype(jnp.float32) * scale
    qpos = jnp.arange(nb)[:, None] * BLOCK + jnp.arange(BLOCK)[None, :]
    kpos = idx - BLOCK
    mask = (jnp.abs(qpos[:, :, None] - kpos[:, None, :]) <= B_WINDOW) & (kpos[:, None, :] >= 0) & (kpos[:, None, :] < s)
    sc = jnp.where(mask[None, :, None, None, :, :], sc, jnp.float32(-1e30))
    sink_l = jnp.broadcast_to(sink.astype(jnp.float32).reshape(B_KV_HEADS, rep)[None, None, :, :, None, None],
                              sc.shape[:-1] + (1,))
    p = jax.nn.softmax(jnp.concatenate([sc, sink_l], axis=-1), axis=-1)[..., :-1]
    o = jnp.einsum('bnhrqk,bnkhd->bnqhrd', p.astype(vb.dtype), vb)
    return o.reshape(bn, s, B_WIDTH)


def mixer_c(q, k, v, q_norm, k_norm, row, col):
    bn, s, _ = q.shape
    nb = s // BLOCK
    rep = C_HEADS // C_KV_HEADS
    q = axial_rope(rms_norm(q.reshape(bn, s, C_HEADS, HEAD_DIM), q_norm), row, col)
    k = axial_rope(rms_norm(k.reshape(bn, s, C_KV_HEADS, HEAD_DIM), k_norm), row, col)
    v = v.reshape(bn, s, C_KV_HEADS, HEAD_DIM)
    qb = q.reshape(bn, nb, BLOCK, C_KV_HEADS, rep, HEAD_DIM).transpose(1, 0, 2, 3, 4, 5)
    scale = HEAD_DIM ** -0.5

    def attend(qblk):
        sc = jnp.einsum('bqhrd,bkhd->bhrqk', qblk, k).astype(jnp.float32) * scale
        p = jax.nn.softmax(sc, axis=-1).astype(v.dtype)
        return jnp.einsum('bhrqk,bkhd->bqhrd', p, v)

    o = lax.map(attend, qb)
    return o.transpose(1, 0, 2, 3, 4, 5).reshape(bn, s, C_WIDTH)


def even_layer(x, norm_g, w_in, w_out, a_vn, a_ws, a_bs, b_qn, b_kn, b_sink, pos):
    h = rms_norm(x, norm_g)
    z = h @ w_in
    sizes = [A_WIDTH, A_WIDTH, A_WIDTH, B_WIDTH, B_KV_HEADS * HEAD_DIM, B_KV_HEADS * HEAD_DIM, B_WIDTH]
    a_u, a_v, a_g, b_q, b_k, b_v, b_g = jnp.split(z, _split_points(sizes), axis=-1)
    ya = mixer_a(jax.nn.gelu(a_u), jax.nn.gelu(a_v), a_vn, a_ws, a_bs) * jax.nn.silu(a_g)
    yb = mixer_b(b_q, b_k, b_v, b_qn, b_kn, b_sink, pos) * jax.nn.silu(b_g)
    return x + jnp.concatenate([ya, yb], axis=-1) @ w_out


def odd_layer(x, norm_g, w_in, w_out, c_qn, c_kn, row, col):
    h = rms_norm(x, norm_g)
    z = h @ w_in
    sizes = [C_WIDTH, C_KV_HEADS * HEAD_DIM, C_KV_HEADS * HEAD_DIM, C_WIDTH]
    c_q, c_k, c_v, c_g = jnp.split(z, _split_points(sizes), axis=-1)
    yc = mixer_c(c_q, c_k, c_v, c_qn, c_kn, row, col) * jax.nn.silu(c_g)
    return x + yc @ w_out


def trunk(x, norm_ab, w_in_ab, w_out_ab, a_v_norm, a_w_s, a_b_s, b_q_norm, b_k_norm, b_sink,
          norm_c, w_in_c, w_out_c, c_q_norm, c_k_norm):
    s = x.shape[1]
    rows = s // GRID_W
    pos = jnp.arange(s, dtype=jnp.float32)
    rr, cc = jnp.meshgrid(jnp.arange(rows, dtype=jnp.float32), jnp.arange(GRID_W, dtype=jnp.float32), indexing='ij')
    row, col = rr.reshape(-1), cc.reshape(-1)
    for layer in range(DEPTH):
        i = layer // 2
        if layer % 2 == 0:
            x = even_layer(x, norm_ab[i], w_in_ab[i], w_out_ab[i], a_v_norm[i], a_w_s[i], a_b_s[i],
                           b_q_norm[i], b_k_norm[i], b_sink[i], pos)
        else:
            x = odd_layer(x, norm_c[i], w_in_c[i], w_out_c[i], c_q_norm[i], c_k_norm[i], row, col)
    return x


def setup_inputs(seed: int = 0) -> dict:
    key = jax.random.key(seed)
    ks = jax.random.split(key, 20)
    f32 = jnp.float32
    nrm = lambda k, shape, sc: jax.random.normal(k, shape, f32) * sc
    gain = lambda k, shape: 1.0 + 0.02 * jax.random.normal(k, shape, f32)
    return {
        "x_prompt": jax.random.normal(ks[0], (BATCH, SEQ, D_MODEL), f32),
        "x_sample": jax.random.normal(ks[1], (DEC_BATCH, DEC_SEQ, D_MODEL), f32),
        "norm_ab": gain(ks[2], (N_EVEN, D_MODEL)),
        "w_in_ab": nrm(ks[3], (N_EVEN, D_MODEL, AB_IN), D_MODEL ** -0.5),
        "w_out_ab": nrm(ks[4], (N_EVEN, A_WIDTH + B_WIDTH, D_MODEL), 0.5 * (A_WIDTH + B_WIDTH) ** -0.5),
        "a_v_norm": gain(ks[5], (N_EVEN, A_WIDTH)),
        "a_w_s": nrm(ks[6], (N_EVEN, A_GROUPS, A_CHUNK, A_CHUNK), 0.5 * A_CHUNK ** -0.5),
        "a_b_s": 1.0 + nrm(ks[7], (N_EVEN, A_GROUPS, A_CHUNK), 0.01),
        "b_q_norm": gain(ks[8], (N_EVEN, HEAD_DIM)),
        "b_k_norm": gain(ks[9], (N_EVEN, HEAD_DIM)),
        "b_sink": nrm(ks[10], (N_EVEN, B_HEADS), 0.5),
        "norm_c": gain(ks[11], (N_ODD, D_MODEL)),
        "w_in_c": nrm(ks[12], (N_ODD, D_MODEL, C_IN), D_MODEL ** -0.5),
        "w_out_c": nrm(ks[13], (N_ODD, C_WIDTH, D_MODEL), 0.5 * C_WIDTH ** -0.5),
        "c_q_norm": gain(ks[14], (N_ODD, HEAD_DIM)),
        "c_k_norm": gain(ks[15], (N_ODD, HEAD_DIM)),
    }


def reference(x_prompt, x_sample, norm_ab, w_in_ab, w_out_ab, a_v_norm, a_w_s, a_b_s, b_q_norm, b_k_norm,
              b_sink, norm_c, w_in_c, w_out_c, c_q_norm, c_k_norm):
    y_prompt = trunk(x_prompt, norm_ab, w_in_ab, w_out_ab, a_v_norm, a_w_s, a_b_s, b_q_norm, b_k_norm, b_sink,
                     norm_c, w_in_c, w_out_c, c_q_norm, c_k_norm)
    y_sample = trunk(x_sample, norm_ab, w_in_ab, w_out_ab, a_v_norm, a_w_s, a_b_s, b_q_norm, b_k_norm, b_sink,
                     norm_c, w_in_c, w_out_c, c_q_norm, c_k_norm)
    return (y_prompt, y_sample)
```

```python
import numpy as np
import concourse.bass as bass
import concourse.mybir as mybir
from concourse.bass_utils import run_bass_kernel_spmd

F32 = mybir.dt.float32
BF16 = mybir.dt.bfloat16
AF = mybir.ActivationFunctionType
ALU = mybir.AluOpType
AX = mybir.AxisListType
DTSIZE = {F32: 4, BF16: 2}

D = 2048
KC = 16
EPS = 1e-6
HD = 128
ATT_SCALE = float(HD ** -0.5)
AB_IN = 5632
C_IN = 5120

SAME_ENGINE_RAW_SYNC = True


class DSem:
    __slots__ = ("sem", "cnt")

    def __init__(self, sem):
        self.sem = sem
        self.cnt = 0


class Tok:
    __slots__ = ("name", "last_w", "readers", "dsem", "persistent")

    def __init__(self, name, persistent=False):
        self.name = name
        self.last_w = None
        self.readers = {}
        self.dsem = {}
        self.persistent = persistent


class EngQ:
    def __init__(self, name):
        self.name = name
        self.ops = []
        self.seen_eng = {}
        self.seen_dma = {}
        self.sem = None


class FW:
    def __init__(self, nc):
        self.nc = nc
        self.q = {n: EngQ(n) for n in ("pe", "act", "dve", "pool", "sp")}
        for n, e in self.q.items():
            e.sem = nc.alloc_semaphore(name=f"sem_{n}")
        self.toks = []
        self.nsem = 5
        self.free_dsems = {"sp": [], "pool": [], "act": []}
        self.all_dsems = []

    def tok(self, name, persistent=False):
        t = Tok(name, persistent)
        self.toks.append(t)
        return t

    def epoch(self):
        keep = []
        for t in self.toks:
            if t.persistent or t.name.startswith("ps"):
                keep.append(t)
                continue
            for qn, ds in t.dsem.items():
                self.free_dsems[qn].append(ds)
            t.dsem = {}
        self.toks = keep

    def toks_n(self, name, n):
        return [self.tok(f"{name}{i}") for i in range(n)]

    def _need(self, q, ev, is_raw):
        if ev is None:
            return
        if ev[0] == "eng":
            _, pname, idx = ev
            if pname == q.name and not (is_raw and SAME_ENGINE_RAW_SYNC):
                return
            if q.seen_eng.get(pname, -1) >= idx:
                return
            q.seen_eng[pname] = idx
            self.q[pname].ops[idx][2] = True
            q.ops.append(["wait", "eng", pname, idx])
        else:
            _, ds, _v = ev
            val = ds.cnt * 16
            if q.seen_dma.get(id(ds), 0) >= val:
                return
            q.seen_dma[id(ds)] = val
            q.ops.append(["wait", "dma", ds, val])

    def _deps(self, q, reads, writes):
        for t in reads:
            self._need(q, t.last_w, True)
        for t in writes:
            self._need(q, t.last_w, False)
            for ev in t.readers.values():
                self._need(q, ev, False)

    def _commit(self, ev, reads, writes):
        key = (ev[0], ev[1] if ev[0] == "eng" else id(ev[1]))
        for t in reads:
            t.readers[key] = ev
        for t in writes:
            t.last_w = ev
            t.readers = {}

    def op(self, eng, fn, reads=(), writes=()):
        q = self.q[eng]
        self._deps(q, reads, writes)
        idx = len(q.ops)
        q.ops.append(["op", fn, False])
        ev = ("eng", eng, idx)
        self._commit(ev, reads, writes)
        return ev

    def dma(self, eng, out, in_, reads=(), writes=(), **kw):
        q = self.q[eng]
        self._deps(q, reads, writes)
        t = (list(writes) + list(reads))[0]
        if eng not in t.dsem:
            if self.free_dsems[eng]:
                t.dsem[eng] = self.free_dsems[eng].pop()
            else:
                t.dsem[eng] = DSem(self.nc.alloc_semaphore(name=f"dsem{self.nsem}"))
                self.all_dsems.append(t.dsem[eng])
                self.nsem += 1
        ds = t.dsem[eng]
        ds.cnt += 1
        q.ops.append(["dma", out, in_, ds, kw])
        ev = ("dma", ds, ds.cnt * 16)
        self._commit(ev, reads, writes)
        return ev

    def barrier(self):
        last = {}
        for n, q in self.q.items():
            for i in range(len(q.ops) - 1, -1, -1):
                if q.ops[i][0] == "op":
                    last[n] = i
                    break
        for n, q in self.q.items():
            for pn, idx in last.items():
                if pn == n:
                    continue
                self._need(q, ("eng", pn, idx), True)
            for ds in self.all_dsems:
                if ds.cnt > 0:
                    self._need(q, ("dma", ds, ds.cnt * 16), True)

    def emit(self, block):
        vals = {}
        for n, q in self.q.items():
            c = 0
            v = {}
            for i, o in enumerate(q.ops):
                if o[0] == "op" and o[2]:
                    c += 1
                    v[i] = c
            vals[n] = v
        self.stats = {n: (len(q.ops), len(vals[n])) for n, q in self.q.items()}

        def run(q, e):
            for o in q.ops:
                if o[0] == "op":
                    ins = o[1](e)
                    if o[2]:
                        ins.then_inc(q.sem, 1)
                elif o[0] == "dma":
                    _, out, in_, ds, kw = o
                    e.dma_start(out=out, in_=in_, **kw).then_inc(ds.sem, 16)
                else:
                    if o[1] == "eng":
                        e.wait_ge(self.q[o[2]].sem, vals[o[2]][o[3]])
                    else:
                        e.wait_ge(o[2].sem, o[3])

        @block.tensor
        def _(e):
            run(self.q["pe"], e)

        @block.scalar
        def _(e):
            run(self.q["act"], e)

        @block.vector
        def _(e):
            run(self.q["dve"], e)

        @block.gpsimd
        def _(e):
            run(self.q["pool"], e)

        @block.sync
        def _(e):
            run(self.q["sp"], e)


class Arena:
    def __init__(self, nc, start=16512, limit=229312):
        self.nc = nc
        self.base = start
        self.off = start
        self.limit = limit
        self.n = 0

    def reset(self):
        self.off = self.base

    def alloc(self, name, shape, dtype):
        size = int(np.prod(shape[1:])) * DTSIZE[dtype]
        off = (self.off + 63) // 64 * 64
        assert off + size <= self.limit, (name, off, size)
        self.off = off + size
        self.n += 1
        return self.nc.alloc_sbuf_tensor_at(f"{name}_{self.n}", list(shape), dtype, offset=off)


class Ring:
    def __init__(self, K, name, n, shape, dtype):
        self.bufs = [K.ar.alloc(name, shape, dtype) for _ in range(n)]
        self.toks = K.fw.toks_n(name, n)
        self.i = 0

    def next(self):
        k = self.i % len(self.bufs)
        self.i += 1
        return self.bufs[k], self.toks[k]


class PRing:
    def __init__(self, K, idxs):
        self.bufs = [K.PS[i] for i in idxs]
        self.toks = [K.tPS[i] for i in idxs]
        self.i = 0

    def next(self):
        k = self.i % len(self.bufs)
        self.i += 1
        return self.bufs[k], self.toks[k]


class KCtx:
    pass


def build_program(seq_lens, n_layers=4, debug=False):
    nc = bass.Bass("TRN2", target_bir_lowering=False)
    fw = FW(nc)
    K = KCtx()
    K.nc, K.fw = nc, fw
    K.ar = Arena(nc)
    SMAX = max(seq_lens)
    NBMAX = SMAX // 128

    def din(name, shape, dt=F32):
        return nc.dram_tensor(name, list(shape), dt, kind="ExternalInput").ap()

    def dscr(name, shape, dt=BF16):
        kind = "ExternalOutput" if debug else "Internal"
        return nc.dram_tensor(name, list(shape), dt, kind=kind).ap()

    K.x_in = [din(f"x{i}", [S, D]) for i, S in enumerate(seq_lens)]
    K.y = [nc.dram_tensor(f"y{i}", [S, D], F32, kind="ExternalOutput").ap() for i, S in enumerate(seq_lens)]
    K.ropeE = [din(f"ropeE{i}", [S, 64]) for i, S in enumerate(seq_lens)]
    K.ropeO = [din(f"ropeO{i}", [S, 256]) for i, S in enumerate(seq_lens)]
    K.norm_t = din("norm_t", [128, 4, 16])
    K.w_in_ab = din("w_in_ab", [2, D, AB_IN])
    K.w_out_ab = din("w_out_ab", [2, D, D])
    K.w_in_c = din("w_in_c", [2, D, C_IN])
    K.w_out_c = din("w_out_c", [2, D, D])
    K.wsT = din("wsT", [2, 128, 8, 128])
    K.bbc = din("bbc", [2, 128, 1024])
    K.vgcol = din("vgcol", [2, 128, 8])
    K.sinkbc = din("sinkbc", [2, 128, 8])
    K.gqE = din("gqE", [2, 128, 512])
    K.gkE = din("gkE", [2, 128, 512])
    K.gqO = din("gqO", [2, 128, 512])
    K.gkO = din("gkO", [2, 128, 512])
    K.ident_d = din("ident", [128, 128])
    K.masks_d = din("masks", [128, 2, 512])

    K.wab = [dscr(f"wab{i}", [11, 128, 16, 512]) for i in range(2)]
    K.wc = [dscr(f"wc{i}", [10, 128, 16, 512]) for i in range(2)]
    K.woab = [dscr(f"woab{i}", [128, 16, D]) for i in range(2)]
    K.woc = [dscr(f"woc{i}", [128, 16, D]) for i in range(2)]
    K.s_fm = dscr("s_fm", [3072, SMAX])
    K.s_vn = dscr("s_vn", [SMAX, 1024])
    K.s_qT = dscr("s_qT", [NBMAX, 4, 128, 512])
    K.s_kT = dscr("s_kT", [4, 128, SMAX])
    K.s_v = dscr("s_v", [SMAX, 512])
    K.s_cat = dscr("s_cat", [D, SMAX])

    K.PS = [nc.alloc_psum_tensor(f"ps{i}", [128, 512], F32) for i in range(8)]
    K.tPS = fw.toks_n("ps", 8)

    ar = K.ar
    K.identb = ar.alloc("identb", [128, 128], BF16)
    K.t_identb = fw.tok("identb", True)
    K.onesb = ar.alloc("onesb", [128, 128], BF16)
    K.t_onesb = fw.tok("onesb", True)
    K.maskb = ar.alloc("maskb", [128, 2, 512], BF16)
    K.t_maskb = fw.tok("maskb", True)
    ar.base = ar.off
    fw.dma("pool", K.identb[:], K.ident_d[:, :], writes=[K.t_identb])
    fw.dma("pool", K.maskb[:], K.masks_d[:, :, :], writes=[K.t_maskb])
    fw.op("dve", lambda e: e.memset(K.onesb[:], 1.0), writes=[K.t_onesb])

    prep_weights(K, n_layers)
    fw.barrier()
    fw.epoch()
    for si, S in enumerate(seq_lens):
        for layer in range(n_layers):
            li = layer // 2
            even = layer % 2 == 0
            src = K.x_in[si] if layer == 0 else K.y[si]
            pass1(K, si, S, even, li, src)
            fw.barrier()
            fw.epoch()
            if even:
                pass2_even(K, si, S, li)
            else:
                pass2_odd(K, si, S, li)
            fw.barrier()
            fw.epoch()
            pass3(K, si, S, K.woab[li] if even else K.woc[li], src)
            fw.barrier()
            fw.epoch()
    fw.barrier()
    fw.epoch()
    with nc.Block() as block:
        fw.emit(block)
    return nc, fw


def prep_weights(K, n_layers):
    fw, ar = K.fw, K.ar
    ar.reset()
    wf = Ring(K, "wf", 2, [128, AB_IN], F32)
    wb = Ring(K, "wb", 2, [128, AB_IN], BF16)
    gt = ar.alloc("gt", [128, 4, 16], F32)
    t_gt = fw.tok("gt")
    fw.dma("sp", gt[:], K.norm_t[:, :, :], writes=[t_gt])
    jobs = []
    for i in range(2):
        if 2 * i < n_layers:
            jobs.append((K.w_in_ab[i], AB_IN, i, K.wab[i], True))
            jobs.append((K.w_out_ab[i], D, None, K.woab[i], False))
        if 2 * i + 1 < n_layers:
            jobs.append((K.w_in_c[i], C_IN, 2 + i, K.wc[i], True))
            jobs.append((K.w_out_c[i], D, None, K.woc[i], False))
    n = 0
    for src, N, gi, dst, is_in in jobs:
        for c in range(KC):
            f, tf = wf.next()
            b, tb = wb.next()
            fw.dma("sp", f[:, :N], src[c * 128:(c + 1) * 128, :], writes=[tf])
            if gi is not None:
                if n % 2 == 0:
                    fw.op("act", lambda e, b=b, f=f, N=N, gi=gi, c=c: e.activation(
                        out=b[:, :N], in_=f[:, :N], func=AF.Copy, scale=gt[:, gi, c:c + 1]),
                        reads=[tf, t_gt], writes=[tb])
                else:
                    fw.op("dve", lambda e, b=b, f=f, N=N, gi=gi, c=c: e.tensor_scalar_mul(
                        out=b[:, :N], in0=f[:, :N], scalar1=gt[:, gi, c:c + 1]),
                        reads=[tf, t_gt], writes=[tb])
            else:
                if n % 2 == 0:
                    fw.op("act", lambda e, b=b, f=f, N=N: e.activation(out=b[:, :N], in_=f[:, :N], func=AF.Copy),
                          reads=[tf], writes=[tb])
                else:
                    fw.op("dve", lambda e, b=b, f=f, N=N: e.tensor_copy(out=b[:, :N], in_=f[:, :N]),
                          reads=[tf], writes=[tb])
            if is_in:
                fw.dma("pool", dst[:, :, c, :].rearrange("t p n -> p t n"),
                       b[:, :N].rearrange("p (t n) -> p t n", n=512), reads=[tb])
            else:
                fw.dma("pool", dst[:, c, :], b[:, :N], reads=[tb])
            n += 1


def pass1(K, si, S, even, li, x_src):
    fw, ar, nc = K.fw, K.ar, K.nc
    ar.reset()
    nb = S // 128
    SBT = 8
    assert nb % SBT == 0
    nsb = nb // SBT
    ST = SBT * 128
    NT = 11 if even else 10
    wsrc = K.wab[li] if even else K.wc[li]
    RW = 64 if even else 256
    rope_d = K.ropeE[si] if even else K.ropeO[si]

    xin = Ring(K, "xin", 2, [128, D], F32)
    junk = ar.alloc("junk", [128, D], BF16)
    t_junk = fw.tok("junk")
    hb = Ring(K, "hb", 2, [128, D], BF16)
    st0 = Ring(K, "st0", 2, [128, 4], F32)
    hT = ar.alloc("hT", [128, KC, ST], BF16)
    t_hT = fw.toks_n("hT", SBT)
    wt = Ring(K, "wt", 2, [128, KC, 512], BF16)
    zs = Ring(K, "zs", 2, [128, SBT, 512], F32)
    sq = ar.alloc("sq", [128, 512], F32)
    t_sq = fw.tok("sq")
    ssq = Ring(K, "ssq", 2, [128, SBT, 4], F32)
    rr = Ring(K, "rr", 2, [128, 512], F32)
    ob = Ring(K, "ob", 3, [128, 512], BF16)
    obT = Ring(K, "obT", 2, [128, 4, 128], BF16)
    rp = ar.alloc("rp", [128, SBT, RW], F32)
    t_rp = fw.tok("rp")
    gq = ar.alloc("gq", [128, 512], F32)
    gk = ar.alloc("gk", [128, 512], F32)
    t_g = fw.tok("gqk")
    fw.dma("sp", gq[:], (K.gqE if even else K.gqO)[li], writes=[t_g])
    fw.dma("sp", gk[:], (K.gkE if even else K.gkO)[li], writes=[t_g])

    ps_tr = PRing(K, [0, 1])
    ps_mm = PRing(K, [2, 3, 4, 5])
    ps_t2 = PRing(K, [6, 7])

    if even:
        order = [0, 1, 2, 3, 6, 7, 8, 4, 5, 9, 10]
        kinds = {0: "gelu_fm", 1: "gelu_fm", 2: "vn", 3: "vn", 4: "silu_fm", 5: "silu_fm",
                 6: "q", 7: "q", 8: "kv", 9: "silu_fm", 10: "silu_fm"}
        fm_row = {0: 0, 1: 512, 4: 1024, 5: 1536, 9: 2048, 10: 2560}
    else:
        order = list(range(10))
        kinds = {0: "q", 1: "q", 2: "q", 3: "q", 4: "k", 5: "v", 6: "silu_fm", 7: "silu_fm",
                 8: "silu_fm", 9: "silu_fm"}
        fm_row = {6: 0, 7: 512, 8: 1024, 9: 1536}

    for sb in range(nsb):
        tok0 = sb * ST
        fw.dma("sp", rp[:], rope_d[tok0:tok0 + ST, :].rearrange("(b p) w -> p b w", p=128), writes=[t_rp])
        for tb in range(SBT):
            r0 = tok0 + tb * 128
            x, tx = xin.next()
            h, th = hb.next()
            s0, ts0 = st0.next()
            fw.dma("sp", x[:], x_src[r0:r0 + 128, :], writes=[tx])
            fw.op("act", lambda e, x=x, s0=s0: e.activation(out=junk[:], in_=x[:], func=AF.Square,
                                                           scale=float(D ** -0.5), accum_out=s0[:, 0:1]),
                  reads=[tx], writes=[t_junk, ts0])
            fw.op("dve", lambda e, s0=s0: e.tensor_scalar_add(out=s0[:, 1:2], in0=s0[:, 0:1], scalar1=EPS),
                  reads=[ts0], writes=[ts0])
            fw.op("act", lambda e, s0=s0: e.activation(out=s0[:, 2:3], in_=s0[:, 1:2], func=AF.Sqrt),
                  reads=[ts0], writes=[ts0])
            fw.op("dve", lambda e, s0=s0: e.reciprocal(out=s0[:, 3:4], in_=s0[:, 2:3]), reads=[ts0], writes=[ts0])
            fw.op("dve", lambda e, x=x, h=h, s0=s0: e.tensor_scalar_mul(out=h[:], in0=x[:], scalar1=s0[:, 3:4]),
                  reads=[tx, ts0], writes=[th])
            for half in range(2):
                p, tp = ps_tr.next()
                pv = p[:].bitcast(BF16)
                for j in range(8):
                    c = half * 8 + j
                    fw.op("pe", lambda e, pv=pv, h=h, c=c, j=j: e.transpose(
                        out=pv[:, j * 128:(j + 1) * 128], in_=h[:, c * 128:(c + 1) * 128], identity=K.identb[:]),
                        reads=[th, K.t_identb], writes=[tp])
                fw.op("act", lambda e, pv=pv, half=half, tb=tb: e.activation(
                    out=hT[:, half * 8:(half + 1) * 8, tb * 128:(tb + 1) * 128],
                    in_=pv.rearrange("p (c t) -> p c t", c=8), func=AF.Copy),
                    reads=[tp], writes=[t_hT[tb]])

        pending = []
        for nt in order:
            kind = kinds[nt]
            w, tw = wt.next()
            fw.dma("sp", w[:], wsrc[nt], writes=[tw])
            if kind in ("gelu_fm", "silu_fm"):
                func = AF.Gelu_apprx_tanh if kind == "gelu_fm" else AF.Silu
                for fc in range(4):
                    for half in range(ST // 512):
                        p, tp = ps_mm.next()
                        for c in range(KC):
                            fw.op("pe", lambda e, p=p, w=w, c=c, fc=fc, half=half: e.matmul(
                                p[:], lhsT=w[:, c, fc * 128:(fc + 1) * 128],
                                rhs=hT[:, c, half * 512:(half + 1) * 512], start=(c == 0), stop=(c == KC - 1)),
                                reads=[tw] + t_hT[half * 4:(half + 1) * 4], writes=[tp])
                        o, to = ob.next()
                        fw.op("act", lambda e, o=o, p=p, func=func: e.activation(out=o[:], in_=p[:], func=func),
                              reads=[tp], writes=[to])
                        r = fm_row[nt] + fc * 128
                        fw.dma("pool", K.s_fm[r:r + 128, tok0 + half * 512:tok0 + (half + 1) * 512], o[:],
                               reads=[to])
                for fn in pending:
                    fn()
                pending = []
                continue
            z, tz = zs.next()
            sst, tss = ssq.next()
            nh = {"vn": 4, "q": 4, "k": 4, "kv": 2, "v": 0}[kind]
            for tb in range(SBT):
                p, tp = ps_mm.next()
                for c in range(KC):
                    fw.op("pe", lambda e, p=p, w=w, c=c, tb=tb: e.matmul(
                        p[:], lhsT=hT[:, c, tb * 128:(tb + 1) * 128], rhs=w[:, c, :],
                        start=(c == 0), stop=(c == KC - 1)),
                        reads=[tw, t_hT[tb]], writes=[tp])
                r0 = tok0 + tb * 128
                if kind == "v":
                    o, to = ob.next()
                    fw.op("act", lambda e, o=o, p=p: e.activation(out=o[:], in_=p[:], func=AF.Copy),
                          reads=[tp], writes=[to])
                    fw.dma("pool", K.s_v[r0:r0 + 128, 0:512], o[:], reads=[to])
                    continue
                if kind == "kv":
                    o, to = ob.next()
                    fw.op("act", lambda e, o=o, p=p: e.activation(out=o[:, 256:512], in_=p[:, 256:512], func=AF.Copy),
                          reads=[tp], writes=[to])
                    fw.dma("pool", K.s_v[r0:r0 + 128, 0:256], o[:, 256:512], reads=[to])
                func = AF.Gelu_apprx_tanh if kind == "vn" else AF.Copy
                w_ = nh * 128
                fw.op("act", lambda e, z=z, p=p, tb=tb, func=func, w_=w_: e.activation(
                    out=z[:, tb, 0:w_], in_=p[:, 0:w_], func=func), reads=[tp], writes=[tz])
                fw.op("dve", lambda e, z=z, tb=tb, w_=w_: e.tensor_tensor(
                    out=sq[:, 0:w_], in0=z[:, tb, 0:w_], in1=z[:, tb, 0:w_], op=ALU.mult),
                    reads=[tz], writes=[t_sq])
                fw.op("dve", lambda e, sst=sst, tb=tb, nh=nh, w_=w_: e.tensor_reduce(
                    out=sst[:, tb, 0:nh], in_=sq[:, 0:w_].rearrange("p (h d) -> p h d", d=128),
                    axis=AX.X, op=ALU.add), reads=[t_sq], writes=[tss])

            if kind == "v":
                for fn in pending:
                    fn()
                pending = []
                continue

            def finish(kind=kind, nt=nt, z=z, tz=tz, sst=sst, tss=tss, nh=nh, tok0=tok0):
                fw.op("dve", lambda e: e.tensor_scalar(out=sst[:, :, 0:nh], in0=sst[:, :, 0:nh], scalar1=1.0 / 128,
                                                       scalar2=EPS, op0=ALU.mult, op1=ALU.add),
                      reads=[tss], writes=[tss])
                fw.op("act", lambda e: e.activation(out=sst[:, :, 0:nh], in_=sst[:, :, 0:nh], func=AF.Sqrt),
                      reads=[tss], writes=[tss])
                fw.op("dve", lambda e: e.reciprocal(out=sst[:, :, 0:nh], in_=sst[:, :, 0:nh]),
                      reads=[tss], writes=[tss])
                w_ = nh * 128
                for tb in range(SBT):
                    r0 = tok0 + tb * 128
                    if kind == "vn":
                        o, to = ob.next()
                        fw.op("dve", lambda e, o=o, tb=tb: e.tensor_tensor(
                            out=o[:].rearrange("p (h d) -> p h d", d=128),
                            in0=z[:, tb, :].rearrange("p (h d) -> p h d", d=128),
                            in1=sst[:, tb, 0:4].unsqueeze(2).broadcast_to([128, 4, 128]), op=ALU.mult),
                            reads=[tz, tss], writes=[to])
                        fw.dma("pool", K.s_vn[r0:r0 + 128, (nt - 2) * 512:(nt - 1) * 512], o[:], reads=[to])
                        continue
                    g = gq if kind == "q" else gk
                    for h in range(nh):
                        fw.op("dve", lambda e, tb=tb, h=h, g=g: e.scalar_tensor_tensor(
                            out=z[:, tb, h * 128:(h + 1) * 128], in0=z[:, tb, h * 128:(h + 1) * 128],
                            scalar=sst[:, tb, h:h + 1], in1=g[:, h * 128:(h + 1) * 128],
                            op0=ALU.mult, op1=ALU.mult), reads=[tz, tss, t_g], writes=[tz])
                    r, tr = rr.next()
                    o, to = ob.next()
                    if even:
                        xv = z[:, tb, 0:w_].rearrange("p (h d) -> p h d", d=128)
                        rv = r[:, 0:nh * 32].rearrange("p (h d) -> p h d", d=32)
                        cb = rp[:, tb:tb + 1, 0:32].broadcast_to([128, nh, 32])
                        sb_lo = rp[:, tb:tb + 1, 32:48].broadcast_to([128, nh, 16])
                        sb_hi = rp[:, tb:tb + 1, 48:64].broadcast_to([128, nh, 16])
                        fw.op("dve", lambda e, xv=xv, rv=rv, sb_lo=sb_lo: e.tensor_tensor(
                            out=rv[:, :, 0:16], in0=xv[:, :, 16:32], in1=sb_lo, op=ALU.mult),
                            reads=[tz, t_rp], writes=[tr])
                        fw.op("dve", lambda e, xv=xv, rv=rv, sb_hi=sb_hi: e.tensor_tensor(
                            out=rv[:, :, 16:32], in0=xv[:, :, 0:16], in1=sb_hi, op=ALU.mult),
                            reads=[tz, t_rp], writes=[tr])
                        fw.op("dve", lambda e, xv=xv, cb=cb: e.tensor_tensor(
                            out=xv[:, :, 0:32], in0=xv[:, :, 0:32], in1=cb, op=ALU.mult),
                            reads=[tz, t_rp], writes=[tz])
                        fw.op("dve", lambda e, xv=xv, rv=rv: e.tensor_tensor(
                            out=xv[:, :, 0:32], in0=xv[:, :, 0:32], in1=rv, op=ALU.add),
                            reads=[tz, tr], writes=[tz])
                        fw.op("act", lambda e, o=o, tb=tb, w_=w_: e.activation(
                            out=o[:, 0:w_], in_=z[:, tb, 0:w_], func=AF.Copy), reads=[tz], writes=[to])
                    else:
                        xv = z[:, tb, :].rearrange("p (h a t d) -> p h a t d", h=4, a=2, t=2)
                        rv = r[:].rearrange("p (h a t d) -> p h a t d", h=4, a=2, t=2)
                        cb = rp[:, tb:tb + 1, 0:128].broadcast_to([128, 4, 128])
                        sv = rp[:, tb:tb + 1, 128:256].rearrange("p o (a t d) -> p o a t d", a=2, t=2)
                        s_lo = sv[:, :, :, 0, :].broadcast_to([128, 4, 2, 32])
                        s_hi = sv[:, :, :, 1, :].broadcast_to([128, 4, 2, 32])
                        fw.op("dve", lambda e, xv=xv, rv=rv, s_lo=s_lo: e.tensor_tensor(
                            out=rv[:, :, :, 0, :], in0=xv[:, :, :, 1, :], in1=s_lo, op=ALU.mult),
                            reads=[tz, t_rp], writes=[tr])
                        fw.op("dve", lambda e, xv=xv, rv=rv, s_hi=s_hi: e.tensor_tensor(
                            out=rv[:, :, :, 1, :], in0=xv[:, :, :, 0, :], in1=s_hi, op=ALU.mult),
                            reads=[tz, t_rp], writes=[tr])
                        fw.op("dve", lambda e, tb=tb, cb=cb: e.tensor_tensor(
                            out=z[:, tb, :].rearrange("p (h d) -> p h d", d=128),
                            in0=z[:, tb, :].rearrange("p (h d) -> p h d", d=128), in1=cb, op=ALU.mult),
                            reads=[tz, t_rp], writes=[tz])
                        fw.op("dve", lambda e, o=o, tb=tb, r=r: e.tensor_tensor(
                            out=o[:], in0=z[:, tb, :], in1=r[:], op=ALU.add), reads=[tz, tr], writes=[to])
                    p, tp = ps_t2.next()
                    pv = p[:].bitcast(BF16)
                    for h in range(nh):
                        fw.op("pe", lambda e, pv=pv, o=o, h=h: e.transpose(
                            out=pv[:, h * 128:(h + 1) * 128], in_=o[:, h * 128:(h + 1) * 128], identity=K.identb[:]),
                            reads=[to, K.t_identb], writes=[tp])
                    oT, toT = obT.next()
                    fw.op("act", lambda e, oT=oT, pv=pv, nh=nh: e.activation(
                        out=oT[:, 0:nh, :], in_=pv[:, 0:nh * 128].rearrange("p (h t) -> p h t", t=128), func=AF.Copy),
                        reads=[tp], writes=[toT])
                    b = r0 // 128
                    if kind == "q":
                        kvg = (nt - 6) if even else nt
                        fw.dma("pool", K.s_qT[b, kvg], oT[:].rearrange("p h t -> p (h t)"), reads=[toT])
                    else:
                        fw.dma("pool", K.s_kT[0:nh, :, r0:r0 + 128].rearrange("k d t -> d k t"), oT[:, 0:nh, :],
                               reads=[toT])

            for fn in pending:
                fn()
            pending = [finish]
        for fn in pending:
            fn()
        pending = []


def pass2_even(K, si, S, li):
    fw, ar = K.fw, K.ar
    ar.reset()
    nb = S // 128
    kT = ar.alloc("kTall", [128, 2, S], BF16)
    t_kT = fw.tok("kTall")
    vA = ar.alloc("vall", [128, nb, 256], BF16)
    t_vA = fw.tok("vall")
    wsT = ar.alloc("wsT", [128, 8, 128], BF16)
    bbc = ar.alloc("bbc", [128, 8, 128], F32)
    vgc = ar.alloc("vgc", [128, 8], F32)
    esk = ar.alloc("esk", [128, 8], F32)
    t_c = fw.tok("p2e_c")
    for kh in range(2):
        fw.dma("sp", kT[:, kh, :], K.s_kT[kh, :, 0:S], writes=[t_kT])
    fw.dma("sp", vA[:], K.s_v[0:S, 0:256].rearrange("(b p) c -> p b c", p=128), writes=[t_vA])
    fw.dma("pool", wsT[:], K.wsT[li], writes=[t_c])
    fw.dma("sp", bbc[:], K.bbc[li].rearrange("p (g q) -> p g q", g=8), writes=[t_c])
    fw.dma("sp", vgc[:], K.vgcol[li], writes=[t_c])
    fw.dma("sp", esk[:], K.sinkbc[li], writes=[t_c])
    fw.op("act", lambda e: e.activation(out=esk[:], in_=esk[:], func=AF.Exp), reads=[t_c], writes=[t_c])

    vn = Ring(K, "vn", 2, [128, 1024], BF16)
    uT = Ring(K, "uT", 2, [128, 8, 128], BF16)
    sa = Ring(K, "sa", 2, [128, 8, 128], BF16)
    sg = Ring(K, "sg", 2, [128, 8, 128], BF16)
    qT = Ring(K, "qT", 2, [128, 2, 512], BF16)
    tmp = Ring(K, "tmp", 2, [128, 512], F32)
    rd = Ring(K, "rd", 2, [128, 512], F32)
    yo = Ring(K, "yo", 3, [128, 4, 128], BF16)
    PT = Ring(K, "PT", 4, [128, 512], BF16)
    ps_a = PRing(K, [0, 1])
    ps_s = PRing(K, [2, 3, 4])
    ps_o = PRing(K, [5])
    ps_d = PRing(K, [6])

    for b in range(nb):
        c0 = b * 128
        v_, tv = vn.next()
        u_, tu = uT.next()
        a_, ta = sa.next()
        g_, tg = sg.next()
        q_, tq = qT.next()
        fw.dma("sp", v_[:], K.s_vn[c0:c0 + 128, :], writes=[tv])
        fw.dma("sp", u_[:], K.s_fm[0:1024, c0:c0 + 128].rearrange("(g d) t -> d g t", d=128), writes=[tu])
        fw.dma("sp", a_[:], K.s_fm[1024:2048, c0:c0 + 128].rearrange("(g d) t -> d g t", d=128), writes=[ta])
        fw.dma("sp", g_[:], K.s_fm[2048:3072, c0:c0 + 128].rearrange("(g d) t -> d g t", d=128), writes=[tg])
        fw.dma("sp", q_[:], K.s_qT[b, 0:2].rearrange("k d n -> d k n"), writes=[tq])
        for g4 in range(2):
            p, tp = ps_a.next()
            for g in range(4):
                G = g4 * 4 + g
                fw.op("pe", lambda e, p=p, v_=v_, g=g, G=G: e.matmul(
                    p[:, g * 128:(g + 1) * 128], lhsT=v_[:, G * 128:(G + 1) * 128], rhs=wsT[:, G, :],
                    start=True, stop=True), reads=[tv, t_c], writes=[tp])
            t_, tt = tmp.next()
            for g in range(4):
                G = g4 * 4 + g
                fw.op("dve", lambda e, t_=t_, p=p, g=g, G=G: e.scalar_tensor_tensor(
                    out=t_[:, g * 128:(g + 1) * 128], in0=p[:, g * 128:(g + 1) * 128], scalar=vgc[:, G:G + 1],
                    in1=bbc[:, G, :], op0=ALU.mult, op1=ALU.add), reads=[tp, t_c], writes=[tt])
            fw.op("dve", lambda e, t_=t_, u_=u_, g4=g4: e.tensor_tensor(
                out=t_[:], in0=t_[:], in1=u_[:, g4 * 4:(g4 + 1) * 4, :].rearrange("p g t -> p (g t)"), op=ALU.mult),
                reads=[tt, tu], writes=[tt])
            o_, to = yo.next()
            fw.op("dve", lambda e, t_=t_, a_=a_, o_=o_, g4=g4: e.tensor_tensor(
                out=o_[:].rearrange("p g t -> p (g t)"), in0=t_[:],
                in1=a_[:, g4 * 4:(g4 + 1) * 4, :].rearrange("p g t -> p (g t)"), op=ALU.mult),
                reads=[tt, ta], writes=[to])
            fw.dma("pool", K.s_cat[g4 * 512:(g4 + 1) * 512, c0:c0 + 128].rearrange("(g d) t -> d g t", d=128),
                   o_[:], reads=[to])
        for kh in range(2):
            js = [j for j in (b - 1, b, b + 1) if 0 <= j < nb]
            pts = []
            for j in js:
                p, tp = ps_s.next()
                fw.op("pe", lambda e, p=p, kh=kh, j=j, q_=q_: e.matmul(
                    p[:], lhsT=kT[:, kh, j * 128:(j + 1) * 128], rhs=q_[:, kh, :], start=True, stop=True),
                    reads=[t_kT, tq], writes=[tp])
                P_, tP = PT.next()
                fw.op("act", lambda e, P_=P_, p=p: e.activation(out=P_[:], in_=p[:], func=AF.Exp, scale=ATT_SCALE),
                      reads=[tp], writes=[tP])
                if j != b:
                    m = 0 if j < b else 1
                    fw.op("pool", lambda e, P_=P_, m=m: e.tensor_tensor(
                        out=P_[:], in0=P_[:], in1=K.maskb[:, m, :], op=ALU.mult),
                        reads=[tP, K.t_maskb], writes=[tP])
                pts.append((j, P_, tP))
            po, tpo = ps_o.next()
            pd, tpd = ps_d.next()
            for n_, (j, P_, tP) in enumerate(pts):
                fw.op("pe", lambda e, po=po, j=j, kh=kh, P_=P_, n_=n_: e.matmul(
                    po[:], lhsT=vA[:, j, kh * 128:(kh + 1) * 128], rhs=P_[:], start=(n_ == 0),
                    stop=(n_ == len(pts) - 1)), reads=[t_vA, tP], writes=[tpo])
            for n_, (j, P_, tP) in enumerate(pts):
                fw.op("pe", lambda e, pd=pd, P_=P_, n_=n_: e.matmul(
                    pd[:], lhsT=K.onesb[:], rhs=P_[:], start=(n_ == 0), stop=(n_ == len(pts) - 1)),
                    reads=[K.t_onesb, tP], writes=[tpd])
            r_, trd = rd.next()
            fw.op("dve", lambda e, r_=r_, pd=pd, kh=kh: e.tensor_tensor(
                out=r_[:].rearrange("p (h t) -> p h t", h=4), in0=pd[:].rearrange("p (h t) -> p h t", h=4),
                in1=esk[:, kh * 4:(kh + 1) * 4].unsqueeze(2).broadcast_to([128, 4, 128]), op=ALU.add),
                reads=[tpd, t_c], writes=[trd])
            fw.op("dve", lambda e, r_=r_: e.reciprocal(out=r_[:], in_=r_[:]), reads=[trd], writes=[trd])
            t_, tt = tmp.next()
            fw.op("dve", lambda e, t_=t_, po=po, r_=r_: e.tensor_tensor(out=t_[:], in0=po[:], in1=r_[:], op=ALU.mult),
                  reads=[tpo, trd], writes=[tt])
            o_, to = yo.next()
            fw.op("dve", lambda e, t_=t_, g_=g_, o_=o_, kh=kh: e.tensor_tensor(
                out=o_[:].rearrange("p g t -> p (g t)"), in0=t_[:],
                in1=g_[:, kh * 4:(kh + 1) * 4, :].rearrange("p g t -> p (g t)"), op=ALU.mult),
                reads=[tt, tg], writes=[to])
            r0 = 1024 + kh * 512
            fw.dma("pool", K.s_cat[r0:r0 + 512, c0:c0 + 128].rearrange("(g d) t -> d g t", d=128), o_[:],
                   reads=[to])


def pass2_odd(K, si, S, li):
    fw, ar = K.fw, K.ar
    ar.reset()
    nb = S // 128
    kT = Ring(K, "kTh", 2, [128, S], BF16)
    vH = Ring(K, "vh", 2, [128, nb, 128], BF16)
    qT = Ring(K, "qT", 3, [128, 512], BF16)
    gT = Ring(K, "gT", 2, [128, 4, 128], BF16)
    PT = Ring(K, "PT", 4, [128, 512], BF16)
    rd = Ring(K, "rd", 2, [128, 512], F32)
    tmp = Ring(K, "tmp", 2, [128, 512], F32)
    yo = Ring(K, "yo", 3, [128, 4, 128], BF16)
    ps_s = PRing(K, [0, 1, 2])
    ps_o = PRing(K, [3, 4])
    ps_d = PRing(K, [5, 6])

    for kh in range(4):
        k_, tk = kT.next()
        v_, tv = vH.next()
        fw.dma("sp", k_[:], K.s_kT[kh, :, 0:S], writes=[tk])
        fw.dma("sp", v_[:], K.s_v[0:S, kh * 128:(kh + 1) * 128].rearrange("(b p) c -> p b c", p=128), writes=[tv])
        for qb in range(nb):
            c0 = qb * 128
            q_, tq = qT.next()
            g_, tg = gT.next()
            fw.dma("sp", q_[:], K.s_qT[qb, kh], writes=[tq])
            fw.dma("sp", g_[:], K.s_fm[kh * 512:(kh + 1) * 512, c0:c0 + 128].rearrange("(g d) t -> d g t", d=128),
                   writes=[tg])
            po, tpo = ps_o.next()
            pd, tpd = ps_d.next()
            cur = None

            def qk(kb, k_=k_, tk=tk, q_=q_, tq=tq):
                p, tp = ps_s.next()
                fw.op("pe", lambda e, p=p, kb=kb: e.matmul(
                    p[:], lhsT=k_[:, kb * 128:(kb + 1) * 128], rhs=q_[:], start=True, stop=True),
                    reads=[tk, tq], writes=[tp])
                P_, tP = PT.next()
                fw.op("act", lambda e, P_=P_, p=p: e.activation(out=P_[:], in_=p[:], func=AF.Exp, scale=ATT_SCALE),
                      reads=[tp], writes=[tP])
                return P_, tP

            nxt = qk(0)
            for kb in range(nb):
                cur = nxt
                if kb + 1 < nb:
                    nxt = qk(kb + 1)
                P_, tP = cur
                fw.op("pe", lambda e, po=po, kb=kb, P_=P_, v_=v_: e.matmul(
                    po[:], lhsT=v_[:, kb, :], rhs=P_[:], start=(kb == 0), stop=(kb == nb - 1)),
                    reads=[tv, tP], writes=[tpo])
                fw.op("pe", lambda e, pd=pd, kb=kb, P_=P_: e.matmul(
                    pd[:], lhsT=K.onesb[:], rhs=P_[:], start=(kb == 0), stop=(kb == nb - 1)),
                    reads=[K.t_onesb, tP], writes=[tpd])
            r_, trd = rd.next()
            fw.op("dve", lambda e, r_=r_, pd=pd: e.reciprocal(out=r_[:], in_=pd[:]), reads=[tpd], writes=[trd])
            t_, tt = tmp.next()
            fw.op("dve", lambda e, t_=t_, po=po, r_=r_: e.tensor_tensor(out=t_[:], in0=po[:], in1=r_[:], op=ALU.mult),
                  reads=[tpo, trd], writes=[tt])
            o_, to = yo.next()
            fw.op("dve", lambda e, t_=t_, g_=g_, o_=o_: e.tensor_tensor(
                out=o_[:].rearrange("p g t -> p (g t)"), in0=t_[:], in1=g_[:].rearrange("p g t -> p (g t)"),
                op=ALU.mult), reads=[tt, tg], writes=[to])
            fw.dma("pool", K.s_cat[kh * 512:(kh + 1) * 512, c0:c0 + 128].rearrange("(g d) t -> d g t", d=128),
                   o_[:], reads=[to])


def pass3(K, si, S, wo_d, x_src):
    fw, ar = K.fw, K.ar
    ar.reset()
    nb = S // 128
    wo = ar.alloc("wo", [128, KC, D], BF16)
    t_wo = fw.tok("wo")
    for c in range(0, KC, 4):
        fw.dma("sp", wo[:, c:c + 4, :], wo_d[:, c:c + 4, :], writes=[t_wo])
    cT = Ring(K, "cT", 2, [128, KC, 128], BF16)
    xr = Ring(K, "xr", 2, [128, D], F32)
    ob = Ring(K, "o3", 2, [128, D], F32)
    ps = PRing(K, list(range(8)))
    y = K.y[si]
    for b in range(nb):
        c0 = b * 128
        c_, tc = cT.next()
        x_, tx = xr.next()
        o_, to = ob.next()
        fw.dma("sp", c_[:], K.s_cat[:, c0:c0 + 128].rearrange("(c p) t -> p c t", p=128), writes=[tc])
        fw.dma("sp", x_[:], x_src[c0:c0 + 128, :], writes=[tx])
        for nt in range(4):
            p, tp = ps.next()
            for c in range(KC):
                fw.op("pe", lambda e, p=p, c_=c_, c=c, nt=nt: e.matmul(
                    p[:], lhsT=c_[:, c, :], rhs=wo[:, c, nt * 512:(nt + 1) * 512], start=(c == 0), stop=(c == KC - 1)),
                    reads=[tc, t_wo], writes=[tp])
            fw.op("dve", lambda e, p=p, o_=o_, x_=x_, nt=nt: e.tensor_tensor(
                out=o_[:, nt * 512:(nt + 1) * 512], in0=p[:], in1=x_[:, nt * 512:(nt + 1) * 512], op=ALU.add),
                reads=[tp, tx], writes=[to])
        fw.dma("pool", y[c0:c0 + 128, :], o_[:], reads=[to])


def _rope_tables(S):
    pos = np.arange(S, dtype=np.float32)
    inv = np.power(np.float32(500000.0), -np.arange(16, dtype=np.float32) * np.float32(2.0 / 32)).astype(np.float32)
    ang = (pos[:, None] * inv[None, :]).astype(np.float32)
    c, s = np.cos(ang).astype(np.float32), np.sin(ang).astype(np.float32)
    ropeE = np.concatenate([c, c, -s, s], axis=1).astype(np.float32)
    row = (np.arange(S) // 64).astype(np.float32)
    col = (np.arange(S) % 64).astype(np.float32)
    inv2 = np.power(np.float32(10000.0), -np.arange(32, dtype=np.float32) * np.float32(2.0 / 64)).astype(np.float32)
    ar_ = (row[:, None] * inv2[None, :]).astype(np.float32)
    ac_ = (col[:, None] * inv2[None, :]).astype(np.float32)
    cr, sr, cc, sc = np.cos(ar_), np.sin(ar_), np.cos(ac_), np.sin(ac_)
    ropeO = np.concatenate([cr, cr, cc, cc, -sr, sr, -sc, sc], axis=1).astype(np.float32)
    return ropeE, ropeO


def _shared_inputs(inp):
    f = lambda a: np.ascontiguousarray(np.asarray(a, dtype=np.float32))
    rep = lambda v, n: np.ascontiguousarray(np.broadcast_to(np.tile(f(v), n)[None, :], (128, 128 * n)))
    m = {}
    nt = np.stack([f(inp["norm_ab"][0]), f(inp["norm_ab"][1]), f(inp["norm_c"][0]), f(inp["norm_c"][1])], 0)
    m["norm_t"] = np.ascontiguousarray(nt.reshape(4, 16, 128).transpose(2, 0, 1))
    m["w_in_ab"] = f(inp["w_in_ab"])
    m["w_out_ab"] = f(inp["w_out_ab"])
    m["w_in_c"] = f(inp["w_in_c"])
    m["w_out_c"] = f(inp["w_out_c"])
    m["wsT"] = np.ascontiguousarray(f(inp["a_w_s"]).transpose(0, 3, 1, 2))
    bs = f(inp["a_b_s"]).reshape(2, 1, 1024)
    m["bbc"] = np.ascontiguousarray(np.broadcast_to(bs, (2, 128, 1024)))
    m["vgcol"] = np.ascontiguousarray(f(inp["a_v_norm"]).reshape(2, 8, 128).transpose(0, 2, 1))
    m["sinkbc"] = np.ascontiguousarray(np.broadcast_to(f(inp["b_sink"]).reshape(2, 1, 8), (2, 128, 8)))
    m["gqE"] = np.stack([rep(inp["b_q_norm"][i], 4) for i in range(2)], 0)
    m["gkE"] = np.stack([rep(inp["b_k_norm"][i], 4) for i in range(2)], 0)
    m["gqO"] = np.stack([rep(inp["c_q_norm"][i], 4) for i in range(2)], 0)
    m["gkO"] = np.stack([rep(inp["c_k_norm"][i], 4) for i in range(2)], 0)
    m["ident"] = np.eye(128, dtype=np.float32)
    kk = np.arange(128)[:, None]
    qq = np.arange(128)[None, :]
    mprev = (qq <= kk).astype(np.float32)
    mnext = (kk <= qq).astype(np.float32)
    m["masks"] = np.ascontiguousarray(np.stack([np.tile(mprev, (1, 4)), np.tile(mnext, (1, 4))], 1))
    return m


_CACHE = {}


def kernel(**inputs):
    xp = np.asarray(inputs["x_prompt"], dtype=np.float32)
    xs = np.asarray(inputs["x_sample"], dtype=np.float32)
    SP, SS = xp.shape[1], xs.shape[1]
    key = (SP, SS)
    if key not in _CACHE:
        _CACHE[key] = build_program([SP, SS])[0]
    nc = _CACHE[key]
    shared = _shared_inputs(inputs)
    rE0, rO0 = _rope_tables(SP)
    rE1, rO1 = _rope_tables(SS)
    in_maps = []
    for c in range(8):
        m = dict(shared)
        m["x0"] = np.ascontiguousarray(xp[c % 4])
        m["x1"] = np.ascontiguousarray(xs[c % 4])
        m["ropeE0"], m["ropeO0"], m["ropeE1"], m["ropeO1"] = rE0, rO0, rE1, rO1
        in_maps.append(m)
    res = run_bass_kernel_spmd(nc, in_maps, core_ids=list(range(8)))
    yp = np.stack([np.asarray(res.results[c]["y0"], dtype=np.float32) for c in range(4)], 0)
    ys = np.stack([np.asarray(res.results[c]["y1"], dtype=np.float32) for c in range(4)], 0)
    return (yp, ys)
```

```python
import numpy as np
import concourse.bass as bass
import concourse.mybir as mybir
from concourse.bass_utils import run_bass_kernel_spmd

F32 = mybir.dt.float32
BF16 = mybir.dt.bfloat16
AF = mybir.ActivationFunctionType
ALU = mybir.AluOpType
AX = mybir.AxisListType
DTSIZE = {F32: 4, BF16: 2}

D = 2048
KC = 16
EPS = 1e-6
HD = 128
ATT_SCALE = float(HD ** -0.5)
AB_IN = 5632
C_IN = 5120

SAME_ENGINE_RAW_SYNC = True


class DSem:
    __slots__ = ("sem", "cnt", "step")

    def __init__(self, sem, step=16):
        self.sem = sem
        self.cnt = 0
        self.step = step


class Tok:
    __slots__ = ("name", "last_w", "readers", "dsem", "persistent")

    def __init__(self, name, persistent=False):
        self.name = name
        self.last_w = None
        self.readers = {}
        self.dsem = {}
        self.persistent = persistent


class EngQ:
    def __init__(self, name):
        self.name = name
        self.ops = []
        self.seen_eng = {}
        self.seen_dma = {}
        self.sem = None


class FW:
    def __init__(self, nc):
        self.nc = nc
        self.q = {n: EngQ(n) for n in ("pe", "act", "dve", "pool", "sp")}
        for n, e in self.q.items():
            e.sem = nc.alloc_semaphore(name=f"sem_{n}")
        self.toks = []
        self.nsem = 5
        self.free_dsems = {"sp": [], "pool": [], "act": []}
        self.all_dsems = []
        self.cc_sem = None

    def tok(self, name, persistent=False):
        t = Tok(name, persistent)
        self.toks.append(t)
        return t

    def epoch(self):
        keep = []
        for t in self.toks:
            if t.persistent or t.name.startswith("ps"):
                keep.append(t)
                continue
            for qn, ds in t.dsem.items():
                self.free_dsems[qn].append(ds)
            t.dsem = {}
        self.toks = keep

    def toks_n(self, name, n):
        return [self.tok(f"{name}{i}") for i in range(n)]

    def _need(self, q, ev, is_raw):
        if ev is None:
            return
        if ev[0] == "eng":
            _, pname, idx = ev
            if pname == q.name and not (is_raw and SAME_ENGINE_RAW_SYNC):
                return
            if q.seen_eng.get(pname, -1) >= idx:
                return
            q.seen_eng[pname] = idx
            self.q[pname].ops[idx][2] = True
            q.ops.append(["wait", "eng", pname, idx])
        else:
            _, ds, _v = ev
            val = ds.cnt * ds.step
            if q.seen_dma.get(id(ds), 0) >= val:
                return
            q.seen_dma[id(ds)] = val
            q.ops.append(["wait", "dma", ds, val])

    def _deps(self, q, reads, writes):
        for t in reads:
            self._need(q, t.last_w, True)
        for t in writes:
            self._need(q, t.last_w, False)
            for ev in t.readers.values():
                self._need(q, ev, False)

    def _commit(self, ev, reads, writes):
        key = (ev[0], ev[1] if ev[0] == "eng" else id(ev[1]))
        for t in reads:
            t.readers[key] = ev
        for t in writes:
            t.last_w = ev
            t.readers = {}

    def op(self, eng, fn, reads=(), writes=()):
        q = self.q[eng]
        self._deps(q, reads, writes)
        idx = len(q.ops)
        q.ops.append(["op", fn, False])
        ev = ("eng", eng, idx)
        self._commit(ev, reads, writes)
        return ev

    def dma(self, eng, out, in_, reads=(), writes=(), **kw):
        q = self.q[eng]
        self._deps(q, reads, writes)
        t = (list(writes) + list(reads))[0]
        if eng not in t.dsem:
            if self.free_dsems[eng]:
                t.dsem[eng] = self.free_dsems[eng].pop()
            else:
                t.dsem[eng] = DSem(self.nc.alloc_semaphore(name=f"dsem{self.nsem}"))
                self.all_dsems.append(t.dsem[eng])
                self.nsem += 1
        ds = t.dsem[eng]
        ds.cnt += 1
        q.ops.append(["dma", out, in_, ds, kw])
        ev = ("dma", ds, ds.cnt * 16)
        self._commit(ev, reads, writes)
        return ev

    def coll(self, groups, in_ap, out_ap, reads=(), writes=()):
        q = self.q["pool"]
        self._deps(q, reads, writes)
        if self.cc_sem is None:
            self.cc_sem = DSem(self.nc.alloc_semaphore(name="cc_sem"), step=1)
            self.all_dsems.append(self.cc_sem)
            self.nsem += 1
        ds = self.cc_sem
        ds.cnt += 1
        q.ops.append(["coll", groups, in_ap, out_ap, ds])
        ev = ("dma", ds, ds.cnt)
        self._commit(ev, reads, writes)
        return ev

    def barrier(self):
        last = {}
        for n, q in self.q.items():
            for i in range(len(q.ops) - 1, -1, -1):
                if q.ops[i][0] == "op":
                    last[n] = i
                    break
        for n, q in self.q.items():
            for pn, idx in last.items():
                if pn == n:
                    continue
                self._need(q, ("eng", pn, idx), True)
            for ds in self.all_dsems:
                if ds.cnt > 0:
                    self._need(q, ("dma", ds, ds.cnt * ds.step), True)

    def emit(self, block):
        vals = {}
        for n, q in self.q.items():
            c = 0
            v = {}
            for i, o in enumerate(q.ops):
                if o[0] == "op" and o[2]:
                    c += 1
                    v[i] = c
            vals[n] = v
        self.stats = {n: (len(q.ops), len(vals[n])) for n, q in self.q.items()}

        def run(q, e):
            for o in q.ops:
                if o[0] == "op":
                    ins = o[1](e)
                    if o[2]:
                        ins.then_inc(q.sem, 1)
                elif o[0] == "dma":
                    _, out, in_, ds, kw = o
                    e.dma_start(out=out, in_=in_, **kw).then_inc(ds.sem, 16)
                elif o[0] == "coll":
                    _, groups, in_ap, out_ap, ds = o
                    e.collective_compute("AllGather", ALU.bypass, replica_groups=groups,
                                         ins=[in_ap], outs=[out_ap]).then_inc(ds.sem)
                else:
                    if o[1] == "eng":
                        e.wait_ge(self.q[o[2]].sem, vals[o[2]][o[3]])
                    else:
                        e.wait_ge(o[2].sem, o[3])

        @block.tensor
        def _(e):
            run(self.q["pe"], e)

        @block.scalar
        def _(e):
            run(self.q["act"], e)

        @block.vector
        def _(e):
            run(self.q["dve"], e)

        @block.gpsimd
        def _(e):
            run(self.q["pool"], e)

        @block.sync
        def _(e):
            run(self.q["sp"], e)


class Arena:
    def __init__(self, nc, start=16512, limit=229312):
        self.nc = nc
        self.base = start
        self.off = start
        self.limit = limit
        self.n = 0

    def reset(self):
        self.off = self.base

    def alloc(self, name, shape, dtype):
        size = int(np.prod(shape[1:])) * DTSIZE[dtype]
        off = (self.off + 63) // 64 * 64
        assert off + size <= self.limit, (name, off, size)
        self.off = off + size
        self.n += 1
        return self.nc.alloc_sbuf_tensor_at(f"{name}_{self.n}", list(shape), dtype, offset=off)


class Ring:
    def __init__(self, K, name, n, shape, dtype):
        self.bufs = [K.ar.alloc(name, shape, dtype) for _ in range(n)]
        self.toks = K.fw.toks_n(name, n)
        self.i = 0

    def next(self):
        k = self.i % len(self.bufs)
        self.i += 1
        return self.bufs[k], self.toks[k]


class PRing:
    def __init__(self, K, idxs):
        self.bufs = [K.PS[i] for i in idxs]
        self.toks = [K.tPS[i] for i in idxs]
        self.i = 0

    def next(self):
        k = self.i % len(self.bufs)
        self.i += 1
        return self.bufs[k], self.toks[k]


class KCtx:
    pass


PAIR_GROUPS = [[0, 1], [2, 3], [4, 5], [6, 7]]


def build_program(seq_lens, n_layers=4, debug=False, pair=False):
    nc = bass.Bass("TRN2", target_bir_lowering=False)
    fw = FW(nc)
    K = KCtx()
    K.nc, K.fw = nc, fw
    K.ar = Arena(nc)
    K.pair = pair
    K.NR = NR = 2 if pair else 1
    full_lens = list(seq_lens)
    seq_lens = [S // NR for S in full_lens]
    SMAX = max(seq_lens)
    NBMAX = SMAX // 128

    def din(name, shape, dt=F32):
        return nc.dram_tensor(name, list(shape), dt, kind="ExternalInput").ap()

    def dscr(name, shape, dt=BF16):
        kind = "ExternalOutput" if debug else "Internal"
        return nc.dram_tensor(name, list(shape), dt, kind=kind).ap()

    K.x_in = [din(f"x{i}", [S, D]) for i, S in enumerate(seq_lens)]
    K.y = [nc.dram_tensor(f"y{i}", [S, D], F32, kind="ExternalOutput").ap() for i, S in enumerate(seq_lens)]
    K.ropeE = [din(f"ropeE{i}", [S, 64]) for i, S in enumerate(seq_lens)]
    K.ropeO = [din(f"ropeO{i}", [S, 256]) for i, S in enumerate(seq_lens)]
    K.norm_t = din("norm_t", [128, 4, 16])
    K.w_in_ab = din("w_in_ab", [2, D, AB_IN])
    K.w_out_ab = din("w_out_ab", [2, D, D])
    K.w_in_c = din("w_in_c", [2, D, C_IN])
    K.w_out_c = din("w_out_c", [2, D, D])
    K.wsT = din("wsT", [2, 128, 8, 128])
    K.bbc = din("bbc", [2, 128, 1024])
    K.vgcol = din("vgcol", [2, 128, 8])
    K.sinkbc = din("sinkbc", [2, 128, 8])
    K.gqE = din("gqE", [2, 128, 512])
    K.gkE = din("gkE", [2, 128, 512])
    K.gqO = din("gqO", [2, 128, 512])
    K.gkO = din("gkO", [2, 128, 512])
    K.ident_d = din("ident", [128, 128])
    K.masks_d = din("masks", [128, 2, 512])
    K.edge_d = din("edge", [128, 2, 512])

    K.wab = [dscr(f"wab{i}", [11, 128, 16, 512]) for i in range(2)]
    K.wc = [dscr(f"wc{i}", [10, 128, 16, 512]) for i in range(2)]
    K.woab = [dscr(f"woab{i}", [128, 16, D]) for i in range(2)]
    K.woc = [dscr(f"woc{i}", [128, 16, D]) for i in range(2)]
    K.s_fm = dscr("s_fm", [3072, SMAX])
    K.s_vn = dscr("s_vn", [SMAX, 1024])
    K.s_qT = dscr("s_qT", [NBMAX, 4, 128, 512])
    K.VS = 1024
    K.kT_loc = [[dscr(f"kT_loc{i}_{kh}", [128, S]) for kh in range(4)] for i, S in enumerate(seq_lens)]
    K.v_loc = [[dscr(f"v_loc{i}_{j}", [K.VS, 512]) for j in range(S // K.VS)] for i, S in enumerate(seq_lens)]
    if pair:
        K.kT_g = [[dscr(f"kT_g{i}_{kh}", [NR * 128, S]) for kh in range(4)] for i, S in enumerate(seq_lens)]
        K.v_g = [[dscr(f"v_g{i}_{j}", [NR * K.VS, 512]) for j in range(S // K.VS)] for i, S in enumerate(seq_lens)]
        K.h_in = dscr("h_in", [1024, 128])
        K.h_g = dscr("h_g", [NR * 1024, 128])
    else:
        K.kT_g, K.v_g = K.kT_loc, K.v_loc
    K.s_cat = dscr("s_cat", [D, SMAX])

    K.PS = [nc.alloc_psum_tensor(f"ps{i}", [128, 512], F32) for i in range(8)]
    K.tPS = fw.toks_n("ps", 8)

    ar = K.ar
    K.identb = ar.alloc("identb", [128, 128], BF16)
    K.t_identb = fw.tok("identb", True)
    K.onesb = ar.alloc("onesb", [128, 128], BF16)
    K.t_onesb = fw.tok("onesb", True)
    K.maskb = ar.alloc("maskb", [128, 2, 512], BF16)
    K.t_maskb = fw.tok("maskb", True)
    K.edgeb = ar.alloc("edgeb", [128, 2, 512], BF16)
    K.t_edgeb = fw.tok("edgeb", True)
    ar.base = ar.off
    fw.dma("pool", K.identb[:], K.ident_d[:, :], writes=[K.t_identb])
    fw.dma("pool", K.maskb[:], K.masks_d[:, :, :], writes=[K.t_maskb])
    fw.dma("pool", K.edgeb[:], K.edge_d[:, :, :], writes=[K.t_edgeb])
    fw.op("dve", lambda e: e.memset(K.onesb[:], 1.0), writes=[K.t_onesb])

    prep_weights(K, n_layers)
    fw.barrier()
    fw.epoch()
    for si, S in enumerate(seq_lens):
        for layer in range(n_layers):
            li = layer // 2
            even = layer % 2 == 0
            src = K.x_in[si] if layer == 0 else K.y[si]
            pass1(K, si, S, even, li, src)
            fw.barrier()
            fw.epoch()
            if pair:
                exchange(K, si, S, even)
                fw.barrier()
                fw.epoch()
            if even:
                pass2_even(K, si, S, li)
            else:
                pass2_odd(K, si, S, li)
            fw.barrier()
            fw.epoch()
            pass3(K, si, S, K.woab[li] if even else K.woc[li], src)
            fw.barrier()
            fw.epoch()
    fw.barrier()
    fw.epoch()
    with nc.Block() as block:
        fw.emit(block)
    return nc, fw


def prep_weights(K, n_layers):
    fw, ar = K.fw, K.ar
    ar.reset()
    wf = Ring(K, "wf", 2, [128, AB_IN], F32)
    wb = Ring(K, "wb", 2, [128, AB_IN], BF16)
    gt = ar.alloc("gt", [128, 4, 16], F32)
    t_gt = fw.tok("gt")
    fw.dma("sp", gt[:], K.norm_t[:, :, :], writes=[t_gt])
    jobs = []
    for i in range(2):
        if 2 * i < n_layers:
            jobs.append((K.w_in_ab[i], AB_IN, i, K.wab[i], True))
            jobs.append((K.w_out_ab[i], D, None, K.woab[i], False))
        if 2 * i + 1 < n_layers:
            jobs.append((K.w_in_c[i], C_IN, 2 + i, K.wc[i], True))
            jobs.append((K.w_out_c[i], D, None, K.woc[i], False))
    n = 0
    for src, N, gi, dst, is_in in jobs:
        for c in range(KC):
            f, tf = wf.next()
            b, tb = wb.next()
            fw.dma("sp", f[:, :N], src[c * 128:(c + 1) * 128, :], writes=[tf])
            if gi is not None:
                if n % 2 == 0:
                    fw.op("act", lambda e, b=b, f=f, N=N, gi=gi, c=c: e.activation(
                        out=b[:, :N], in_=f[:, :N], func=AF.Copy, scale=gt[:, gi, c:c + 1]),
                        reads=[tf, t_gt], writes=[tb])
                else:
                    fw.op("dve", lambda e, b=b, f=f, N=N, gi=gi, c=c: e.tensor_scalar_mul(
                        out=b[:, :N], in0=f[:, :N], scalar1=gt[:, gi, c:c + 1]),
                        reads=[tf, t_gt], writes=[tb])
            else:
                if n % 2 == 0:
                    fw.op("act", lambda e, b=b, f=f, N=N: e.activation(out=b[:, :N], in_=f[:, :N], func=AF.Copy),
                          reads=[tf], writes=[tb])
                else:
                    fw.op("dve", lambda e, b=b, f=f, N=N: e.tensor_copy(out=b[:, :N], in_=f[:, :N]),
                          reads=[tf], writes=[tb])
            if is_in:
                fw.dma("pool", dst[:, :, c, :].rearrange("t p n -> p t n"),
                       b[:, :N].rearrange("p (t n) -> p t n", n=512), reads=[tb])
            else:
                fw.dma("pool", dst[:, c, :], b[:, :N], reads=[tb])
            n += 1


def pass1(K, si, S, even, li, x_src):
    fw, ar, nc = K.fw, K.ar, K.nc
    ar.reset()
    nb = S // 128
    SBT = 8
    assert nb % SBT == 0
    nsb = nb // SBT
    ST = SBT * 128
    NT = 11 if even else 10
    wsrc = K.wab[li] if even else K.wc[li]
    RW = 64 if even else 256
    rope_d = K.ropeE[si] if even else K.ropeO[si]

    xin = Ring(K, "xin", 2, [128, D], F32)
    junk = ar.alloc("junk", [128, D], BF16)
    t_junk = fw.tok("junk")
    hb = Ring(K, "hb", 2, [128, D], BF16)
    st0 = Ring(K, "st0", 2, [128, 4], F32)
    hT = ar.alloc("hT", [128, KC, ST], BF16)
    t_hT = fw.toks_n("hT", SBT)
    wt = Ring(K, "wt", 2, [128, KC, 512], BF16)
    zs = Ring(K, "zs", 2, [128, SBT, 512], F32)
    sq = ar.alloc("sq", [128, 512], F32)
    t_sq = fw.tok("sq")
    ssq = Ring(K, "ssq", 2, [128, SBT, 4], F32)
    rr = Ring(K, "rr", 2, [128, 512], F32)
    ob = Ring(K, "ob", 3, [128, 512], BF16)
    obT = Ring(K, "obT", 2, [128, 4, 128], BF16)
    rp = ar.alloc("rp", [128, SBT, RW], F32)
    t_rp = fw.tok("rp")
    gq = ar.alloc("gq", [128, 512], F32)
    gk = ar.alloc("gk", [128, 512], F32)
    t_g = fw.tok("gqk")
    fw.dma("sp", gq[:], (K.gqE if even else K.gqO)[li], writes=[t_g])
    fw.dma("sp", gk[:], (K.gkE if even else K.gkO)[li], writes=[t_g])

    ps_tr = PRing(K, [0, 1])
    ps_mm = PRing(K, [2, 3, 4, 5])
    ps_t2 = PRing(K, [6, 7])

    if even:
        order = [0, 1, 2, 3, 6, 7, 8, 4, 5, 9, 10]
        kinds = {0: "gelu_fm", 1: "gelu_fm", 2: "vn", 3: "vn", 4: "silu_fm", 5: "silu_fm",
                 6: "q", 7: "q", 8: "kv", 9: "silu_fm", 10: "silu_fm"}
        fm_row = {0: 0, 1: 512, 4: 1024, 5: 1536, 9: 2048, 10: 2560}
    else:
        order = list(range(10))
        kinds = {0: "q", 1: "q", 2: "q", 3: "q", 4: "k", 5: "v", 6: "silu_fm", 7: "silu_fm",
                 8: "silu_fm", 9: "silu_fm"}
        fm_row = {6: 0, 7: 512, 8: 1024, 9: 1536}

    for sb in range(nsb):
        tok0 = sb * ST
        fw.dma("sp", rp[:], rope_d[tok0:tok0 + ST, :].rearrange("(b p) w -> p b w", p=128), writes=[t_rp])
        for tb in range(SBT):
            r0 = tok0 + tb * 128
            x, tx = xin.next()
            h, th = hb.next()
            s0, ts0 = st0.next()
            fw.dma("sp", x[:], x_src[r0:r0 + 128, :], writes=[tx])
            fw.op("act", lambda e, x=x, s0=s0: e.activation(out=junk[:], in_=x[:], func=AF.Square,
                                                           scale=float(D ** -0.5), accum_out=s0[:, 0:1]),
                  reads=[tx], writes=[t_junk, ts0])
            fw.op("dve", lambda e, s0=s0: e.tensor_scalar_add(out=s0[:, 1:2], in0=s0[:, 0:1], scalar1=EPS),
                  reads=[ts0], writes=[ts0])
            fw.op("act", lambda e, s0=s0: e.activation(out=s0[:, 2:3], in_=s0[:, 1:2], func=AF.Sqrt),
                  reads=[ts0], writes=[ts0])
            fw.op("dve", lambda e, s0=s0: e.reciprocal(out=s0[:, 3:4], in_=s0[:, 2:3]), reads=[ts0], writes=[ts0])
            fw.op("dve", lambda e, x=x, h=h, s0=s0: e.tensor_scalar_mul(out=h[:], in0=x[:], scalar1=s0[:, 3:4]),
                  reads=[tx, ts0], writes=[th])
            for half in range(2):
                p, tp = ps_tr.next()
                pv = p[:].bitcast(BF16)
                for j in range(8):
                    c = half * 8 + j
                    fw.op("pe", lambda e, pv=pv, h=h, c=c, j=j: e.transpose(
                        out=pv[:, j * 128:(j + 1) * 128], in_=h[:, c * 128:(c + 1) * 128], identity=K.identb[:]),
                        reads=[th, K.t_identb], writes=[tp])
                fw.op("act", lambda e, pv=pv, half=half, tb=tb: e.activation(
                    out=hT[:, half * 8:(half + 1) * 8, tb * 128:(tb + 1) * 128],
                    in_=pv.rearrange("p (c t) -> p c t", c=8), func=AF.Copy),
                    reads=[tp], writes=[t_hT[tb]])

        pending = []
        for nt in order:
            kind = kinds[nt]
            w, tw = wt.next()
            fw.dma("sp", w[:], wsrc[nt], writes=[tw])
            if kind in ("gelu_fm", "silu_fm"):
                func = AF.Gelu_apprx_tanh if kind == "gelu_fm" else AF.Silu
                for fc in range(4):
                    for half in range(ST // 512):
                        p, tp = ps_mm.next()
                        for c in range(KC):
                            fw.op("pe", lambda e, p=p, w=w, c=c, fc=fc, half=half: e.matmul(
                                p[:], lhsT=w[:, c, fc * 128:(fc + 1) * 128],
                                rhs=hT[:, c, half * 512:(half + 1) * 512], start=(c == 0), stop=(c == KC - 1)),
                                reads=[tw] + t_hT[half * 4:(half + 1) * 4], writes=[tp])
                        o, to = ob.next()
                        fw.op("act", lambda e, o=o, p=p, func=func: e.activation(out=o[:], in_=p[:], func=func),
                              reads=[tp], writes=[to])
                        r = fm_row[nt] + fc * 128
                        fw.dma("pool", K.s_fm[r:r + 128, tok0 + half * 512:tok0 + (half + 1) * 512], o[:],
                               reads=[to])
                for fn in pending:
                    fn()
                pending = []
                continue
            z, tz = zs.next()
            sst, tss = ssq.next()
            nh = {"vn": 4, "q": 4, "k": 4, "kv": 2, "v": 0}[kind]
            for tb in range(SBT):
                p, tp = ps_mm.next()
                for c in range(KC):
                    fw.op("pe", lambda e, p=p, w=w, c=c, tb=tb: e.matmul(
                        p[:], lhsT=hT[:, c, tb * 128:(tb + 1) * 128], rhs=w[:, c, :],
                        start=(c == 0), stop=(c == KC - 1)),
                        reads=[tw, t_hT[tb]], writes=[tp])
                r0 = tok0 + tb * 128
                if kind == "v":
                    o, to = ob.next()
                    fw.op("act", lambda e, o=o, p=p: e.activation(out=o[:], in_=p[:], func=AF.Copy),
                          reads=[tp], writes=[to])
                    fw.dma("pool", K.v_loc[si][r0 // K.VS][r0 % K.VS:r0 % K.VS + 128, 0:512], o[:], reads=[to])
                    continue
                if kind == "kv":
                    o, to = ob.next()
                    fw.op("act", lambda e, o=o, p=p: e.activation(out=o[:, 256:512], in_=p[:, 256:512], func=AF.Copy),
                          reads=[tp], writes=[to])
                    fw.dma("pool", K.v_loc[si][r0 // K.VS][r0 % K.VS:r0 % K.VS + 128, 0:256], o[:, 256:512],
                           reads=[to])
                func = AF.Gelu_apprx_tanh if kind == "vn" else AF.Copy
                w_ = nh * 128
                fw.op("act", lambda e, z=z, p=p, tb=tb, func=func, w_=w_: e.activation(
                    out=z[:, tb, 0:w_], in_=p[:, 0:w_], func=func), reads=[tp], writes=[tz])
                fw.op("dve", lambda e, z=z, tb=tb, w_=w_: e.tensor_tensor(
                    out=sq[:, 0:w_], in0=z[:, tb, 0:w_], in1=z[:, tb, 0:w_], op=ALU.mult),
                    reads=[tz], writes=[t_sq])
                fw.op("dve", lambda e, sst=sst, tb=tb, nh=nh, w_=w_: e.tensor_reduce(
                    out=sst[:, tb, 0:nh], in_=sq[:, 0:w_].rearrange("p (h d) -> p h d", d=128),
                    axis=AX.X, op=ALU.add), reads=[t_sq], writes=[tss])

            if kind == "v":
                for fn in pending:
                    fn()
                pending = []
                continue

            def finish(kind=kind, nt=nt, z=z, tz=tz, sst=sst, tss=tss, nh=nh, tok0=tok0):
                fw.op("dve", lambda e: e.tensor_scalar(out=sst[:, :, 0:nh], in0=sst[:, :, 0:nh], scalar1=1.0 / 128,
                                                       scalar2=EPS, op0=ALU.mult, op1=ALU.add),
                      reads=[tss], writes=[tss])
                fw.op("act", lambda e: e.activation(out=sst[:, :, 0:nh], in_=sst[:, :, 0:nh], func=AF.Sqrt),
                      reads=[tss], writes=[tss])
                fw.op("dve", lambda e: e.reciprocal(out=sst[:, :, 0:nh], in_=sst[:, :, 0:nh]),
                      reads=[tss], writes=[tss])
                w_ = nh * 128
                for tb in range(SBT):
                    r0 = tok0 + tb * 128
                    if kind == "vn":
                        o, to = ob.next()
                        fw.op("dve", lambda e, o=o, tb=tb: e.tensor_tensor(
                            out=o[:].rearrange("p (h d) -> p h d", d=128),
                            in0=z[:, tb, :].rearrange("p (h d) -> p h d", d=128),
                            in1=sst[:, tb, 0:4].unsqueeze(2).broadcast_to([128, 4, 128]), op=ALU.mult),
                            reads=[tz, tss], writes=[to])
                        fw.dma("pool", K.s_vn[r0:r0 + 128, (nt - 2) * 512:(nt - 1) * 512], o[:], reads=[to])
                        continue
                    g = gq if kind == "q" else gk
                    for h in range(nh):
                        fw.op("dve", lambda e, tb=tb, h=h, g=g: e.scalar_tensor_tensor(
                            out=z[:, tb, h * 128:(h + 1) * 128], in0=z[:, tb, h * 128:(h + 1) * 128],
                            scalar=sst[:, tb, h:h + 1], in1=g[:, h * 128:(h + 1) * 128],
                            op0=ALU.mult, op1=ALU.mult), reads=[tz, tss, t_g], writes=[tz])
                    r, tr = rr.next()
                    o, to = ob.next()
                    if even:
                        xv = z[:, tb, 0:w_].rearrange("p (h d) -> p h d", d=128)
                        rv = r[:, 0:nh * 32].rearrange("p (h d) -> p h d", d=32)
                        cb = rp[:, tb:tb + 1, 0:32].broadcast_to([128, nh, 32])
                        sb_lo = rp[:, tb:tb + 1, 32:48].broadcast_to([128, nh, 16])
                        sb_hi = rp[:, tb:tb + 1, 48:64].broadcast_to([128, nh, 16])
                        fw.op("dve", lambda e, xv=xv, rv=rv, sb_lo=sb_lo: e.tensor_tensor(
                            out=rv[:, :, 0:16], in0=xv[:, :, 16:32], in1=sb_lo, op=ALU.mult),
                            reads=[tz, t_rp], writes=[tr])
                        fw.op("dve", lambda e, xv=xv, rv=rv, sb_hi=sb_hi: e.tensor_tensor(
                            out=rv[:, :, 16:32], in0=xv[:, :, 0:16], in1=sb_hi, op=ALU.mult),
                            reads=[tz, t_rp], writes=[tr])
                        fw.op("dve", lambda e, xv=xv, cb=cb: e.tensor_tensor(
                            out=xv[:, :, 0:32], in0=xv[:, :, 0:32], in1=cb, op=ALU.mult),
                            reads=[tz, t_rp], writes=[tz])
                        fw.op("dve", lambda e, xv=xv, rv=rv: e.tensor_tensor(
                            out=xv[:, :, 0:32], in0=xv[:, :, 0:32], in1=rv, op=ALU.add),
                            reads=[tz, tr], writes=[tz])
                        fw.op("act", lambda e, o=o, tb=tb, w_=w_: e.activation(
                            out=o[:, 0:w_], in_=z[:, tb, 0:w_], func=AF.Copy), reads=[tz], writes=[to])
                    else:
                        xv = z[:, tb, :].rearrange("p (h a t d) -> p h a t d", h=4, a=2, t=2)
                        rv = r[:].rearrange("p (h a t d) -> p h a t d", h=4, a=2, t=2)
                        cb = rp[:, tb:tb + 1, 0:128].broadcast_to([128, 4, 128])
                        sv = rp[:, tb:tb + 1, 128:256].rearrange("p o (a t d) -> p o a t d", a=2, t=2)
                        s_lo = sv[:, :, :, 0, :].broadcast_to([128, 4, 2, 32])
                        s_hi = sv[:, :, :, 1, :].broadcast_to([128, 4, 2, 32])
                        fw.op("dve", lambda e, xv=xv, rv=rv, s_lo=s_lo: e.tensor_tensor(
                            out=rv[:, :, :, 0, :], in0=xv[:, :, :, 1, :], in1=s_lo, op=ALU.mult),
                            reads=[tz, t_rp], writes=[tr])
                        fw.op("dve", lambda e, xv=xv, rv=rv, s_hi=s_hi: e.tensor_tensor(
                            out=rv[:, :, :, 1, :], in0=xv[:, :, :, 0, :], in1=s_hi, op=ALU.mult),
                            reads=[tz, t_rp], writes=[tr])
                        fw.op("dve", lambda e, tb=tb, cb=cb: e.tensor_tensor(
                            out=z[:, tb, :].rearrange("p (h d) -> p h d", d=128),
                            in0=z[:, tb, :].rearrange("p (h d) -> p h d", d=128), in1=cb, op=ALU.mult),
                            reads=[tz, t_rp], writes=[tz])
                        fw.op("dve", lambda e, o=o, tb=tb, r=r: e.tensor_tensor(
                            out=o[:], in0=z[:, tb, :], in1=r[:], op=ALU.add), reads=[tz, tr], writes=[to])
                    p, tp = ps_t2.next()
                    pv = p[:].bitcast(BF16)
                    for h in range(nh):
                        fw.op("pe", lambda e, pv=pv, o=o, h=h: e.transpose(
                            out=pv[:, h * 128:(h + 1) * 128], in_=o[:, h * 128:(h + 1) * 128], identity=K.identb[:]),
                            reads=[to, K.t_identb], writes=[tp])
                    oT, toT = obT.next()
                    fw.op("act", lambda e, oT=oT, pv=pv, nh=nh: e.activation(
                        out=oT[:, 0:nh, :], in_=pv[:, 0:nh * 128].rearrange("p (h t) -> p h t", t=128), func=AF.Copy),
                        reads=[tp], writes=[toT])
                    b = r0 // 128
                    if kind == "q":
                        kvg = (nt - 6) if even else nt
                        fw.dma("pool", K.s_qT[b, kvg], oT[:].rearrange("p h t -> p (h t)"), reads=[toT])
                    else:
                        for h in range(nh):
                            fw.dma("pool", K.kT_loc[si][h][:, r0:r0 + 128], oT[:, h, :], reads=[toT])

            for fn in pending:
                fn()
            pending = [finish]
        for fn in pending:
            fn()
        pending = []


def exchange(K, si, S, even):
    fw = K.fw
    t_x = fw.tok("xchg")
    if not even:
        for kh in range(4):
            fw.coll(PAIR_GROUPS, K.kT_loc[si][kh][:, :], K.kT_g[si][kh][:, :], writes=[t_x])
        for j in range(S // K.VS):
            fw.coll(PAIR_GROUPS, K.v_loc[si][j][:, :], K.v_g[si][j][:, :], writes=[t_x])
        return
    kl, vl = K.kT_loc[si], K.v_loc[si]
    for kh in range(2):
        fw.dma("pool", K.h_in[kh * 128:(kh + 1) * 128, :], kl[kh][:, 0:128], writes=[t_x])
        fw.dma("pool", K.h_in[256 + kh * 128:256 + (kh + 1) * 128, :], kl[kh][:, S - 128:S], writes=[t_x])
    fw.dma("pool", K.h_in[512:768, :].rearrange("(t a) b -> t (a b)", a=2), vl[0][0:128, 0:256], writes=[t_x])
    fw.dma("pool", K.h_in[768:1024, :].rearrange("(t a) b -> t (a b)", a=2), vl[-1][K.VS - 128:K.VS, 0:256],
           writes=[t_x])
    fw.coll(PAIR_GROUPS, K.h_in[:, :], K.h_g[:, :], reads=[t_x], writes=[t_x])


def pass2_even(K, si, S, li):
    fw, ar = K.fw, K.ar
    ar.reset()
    nb = S // 128
    NS = nb + 2
    kT = ar.alloc("kTall", [128, 2, NS * 128], BF16)
    t_kT = fw.tok("kTall")
    vA = ar.alloc("vall", [128, NS, 256], BF16)
    t_vA = fw.tok("vall")
    wsT = ar.alloc("wsT", [128, 8, 128], BF16)
    bbc = ar.alloc("bbc", [128, 8, 128], F32)
    vgc = ar.alloc("vgc", [128, 8], F32)
    esk = ar.alloc("esk", [128, 8], F32)
    t_c = fw.tok("p2e_c")
    for kh in range(2):
        fw.dma("sp", kT[:, kh, 128:128 + S], K.kT_loc[si][kh][:, :], writes=[t_kT])
        if K.pair:
            fw.dma("sp", kT[:, kh, 0:128], K.h_g[256 + kh * 128:256 + (kh + 1) * 128, :], writes=[t_kT])
            fw.dma("sp", kT[:, kh, (nb + 1) * 128:(nb + 2) * 128],
                   K.h_g[1024 + kh * 128:1024 + (kh + 1) * 128, :], writes=[t_kT])
    for j in range(S // K.VS):
        fw.dma("sp", vA[:, 1 + j * 8:1 + (j + 1) * 8, :],
               K.v_loc[si][j][:, 0:256].rearrange("(b p) c -> p b c", p=128), writes=[t_vA])
    if K.pair:
        fw.dma("sp", vA[:, 0, :], K.h_g[768:1024, :].rearrange("(t a) b -> t (a b)", a=2), writes=[t_vA])
        fw.dma("sp", vA[:, nb + 1, :], K.h_g[1536:1792, :].rearrange("(t a) b -> t (a b)", a=2), writes=[t_vA])
    fw.dma("pool", wsT[:], K.wsT[li], writes=[t_c])
    fw.dma("sp", bbc[:], K.bbc[li].rearrange("p (g q) -> p g q", g=8), writes=[t_c])
    fw.dma("sp", vgc[:], K.vgcol[li], writes=[t_c])
    fw.dma("sp", esk[:], K.sinkbc[li], writes=[t_c])
    fw.op("act", lambda e: e.activation(out=esk[:], in_=esk[:], func=AF.Exp), reads=[t_c], writes=[t_c])

    vn = Ring(K, "vn", 2, [128, 1024], BF16)
    uT = Ring(K, "uT", 2, [128, 8, 128], BF16)
    sa = Ring(K, "sa", 2, [128, 8, 128], BF16)
    sg = Ring(K, "sg", 2, [128, 8, 128], BF16)
    qT = Ring(K, "qT", 2, [128, 2, 512], BF16)
    tmp = Ring(K, "tmp", 2, [128, 512], F32)
    rd = Ring(K, "rd", 2, [128, 512], F32)
    yo = Ring(K, "yo", 3, [128, 4, 128], BF16)
    PT = Ring(K, "PT", 4, [128, 512], BF16)
    ps_a = PRing(K, [0, 1])
    ps_s = PRing(K, [2, 3, 4])
    ps_o = PRing(K, [5])
    ps_d = PRing(K, [6])

    for b in range(nb):
        c0 = b * 128
        v_, tv = vn.next()
        u_, tu = uT.next()
        a_, ta = sa.next()
        g_, tg = sg.next()
        q_, tq = qT.next()
        fw.dma("sp", v_[:], K.s_vn[c0:c0 + 128, :], writes=[tv])
        fw.dma("sp", u_[:], K.s_fm[0:1024, c0:c0 + 128].rearrange("(g d) t -> d g t", d=128), writes=[tu])
        fw.dma("sp", a_[:], K.s_fm[1024:2048, c0:c0 + 128].rearrange("(g d) t -> d g t", d=128), writes=[ta])
        fw.dma("sp", g_[:], K.s_fm[2048:3072, c0:c0 + 128].rearrange("(g d) t -> d g t", d=128), writes=[tg])
        fw.dma("sp", q_[:], K.s_qT[b, 0:2].rearrange("k d n -> d k n"), writes=[tq])
        for g4 in range(2):
            p, tp = ps_a.next()
            for g in range(4):
                G = g4 * 4 + g
                fw.op("pe", lambda e, p=p, v_=v_, g=g, G=G: e.matmul(
                    p[:, g * 128:(g + 1) * 128], lhsT=v_[:, G * 128:(G + 1) * 128], rhs=wsT[:, G, :],
                    start=True, stop=True), reads=[tv, t_c], writes=[tp])
            t_, tt = tmp.next()
            for g in range(4):
                G = g4 * 4 + g
                fw.op("dve", lambda e, t_=t_, p=p, g=g, G=G: e.scalar_tensor_tensor(
                    out=t_[:, g * 128:(g + 1) * 128], in0=p[:, g * 128:(g + 1) * 128], scalar=vgc[:, G:G + 1],
                    in1=bbc[:, G, :], op0=ALU.mult, op1=ALU.add), reads=[tp, t_c], writes=[tt])
            fw.op("dve", lambda e, t_=t_, u_=u_, g4=g4: e.tensor_tensor(
                out=t_[:], in0=t_[:], in1=u_[:, g4 * 4:(g4 + 1) * 4, :].rearrange("p g t -> p (g t)"), op=ALU.mult),
                reads=[tt, tu], writes=[tt])
            o_, to = yo.next()
            fw.op("dve", lambda e, t_=t_, a_=a_, o_=o_, g4=g4: e.tensor_tensor(
                out=o_[:].rearrange("p g t -> p (g t)"), in0=t_[:],
                in1=a_[:, g4 * 4:(g4 + 1) * 4, :].rearrange("p g t -> p (g t)"), op=ALU.mult),
                reads=[tt, ta], writes=[to])
            fw.dma("pool", K.s_cat[g4 * 512:(g4 + 1) * 512, c0:c0 + 128].rearrange("(g d) t -> d g t", d=128),
                   o_[:], reads=[to])
        for kh in range(2):
            lo, hi = (0, NS - 1) if K.pair else (1, NS - 2)
            js = [j for j in (b, b + 1, b + 2) if lo <= j <= hi]
            pts = []
            for j in js:
                p, tp = ps_s.next()
                fw.op("pe", lambda e, p=p, kh=kh, j=j, q_=q_: e.matmul(
                    p[:], lhsT=kT[:, kh, j * 128:(j + 1) * 128], rhs=q_[:, kh, :], start=True, stop=True),
                    reads=[t_kT, tq], writes=[tp])
                P_, tP = PT.next()
                fw.op("act", lambda e, P_=P_, p=p: e.activation(out=P_[:], in_=p[:], func=AF.Exp, scale=ATT_SCALE),
                      reads=[tp], writes=[tP])
                if j != b + 1:
                    m = 0 if j < b + 1 else 1
                    halo = (j == 0) or (j == NS - 1)
                    mk, tmk = (K.edgeb, K.t_edgeb) if halo else (K.maskb, K.t_maskb)
                    fw.op("pool", lambda e, P_=P_, m=m, mk=mk: e.tensor_tensor(
                        out=P_[:], in0=P_[:], in1=mk[:, m, :], op=ALU.mult),
                        reads=[tP, tmk], writes=[tP])
                pts.append((j, P_, tP))
            po, tpo = ps_o.next()
            pd, tpd = ps_d.next()
            for n_, (j, P_, tP) in enumerate(pts):
                fw.op("pe", lambda e, po=po, j=j, kh=kh, P_=P_, n_=n_: e.matmul(
                    po[:], lhsT=vA[:, j, kh * 128:(kh + 1) * 128], rhs=P_[:], start=(n_ == 0),
                    stop=(n_ == len(pts) - 1)), reads=[t_vA, tP], writes=[tpo])
            for n_, (j, P_, tP) in enumerate(pts):
                fw.op("pe", lambda e, pd=pd, P_=P_, n_=n_: e.matmul(
                    pd[:], lhsT=K.onesb[:], rhs=P_[:], start=(n_ == 0), stop=(n_ == len(pts) - 1)),
                    reads=[K.t_onesb, tP], writes=[tpd])
            r_, trd = rd.next()
            fw.op("dve", lambda e, r_=r_, pd=pd, kh=kh: e.tensor_tensor(
                out=r_[:].rearrange("p (h t) -> p h t", h=4), in0=pd[:].rearrange("p (h t) -> p h t", h=4),
                in1=esk[:, kh * 4:(kh + 1) * 4].unsqueeze(2).broadcast_to([128, 4, 128]), op=ALU.add),
                reads=[tpd, t_c], writes=[trd])
            fw.op("dve", lambda e, r_=r_: e.reciprocal(out=r_[:], in_=r_[:]), reads=[trd], writes=[trd])
            t_, tt = tmp.next()
            fw.op("dve", lambda e, t_=t_, po=po, r_=r_: e.tensor_tensor(out=t_[:], in0=po[:], in1=r_[:], op=ALU.mult),
                  reads=[tpo, trd], writes=[tt])
            o_, to = yo.next()
            fw.op("dve", lambda e, t_=t_, g_=g_, o_=o_, kh=kh: e.tensor_tensor(
                out=o_[:].rearrange("p g t -> p (g t)"), in0=t_[:],
                in1=g_[:, kh * 4:(kh + 1) * 4, :].rearrange("p g t -> p (g t)"), op=ALU.mult),
                reads=[tt, tg], writes=[to])
            r0 = 1024 + kh * 512
            fw.dma("pool", K.s_cat[r0:r0 + 512, c0:c0 + 128].rearrange("(g d) t -> d g t", d=128), o_[:],
                   reads=[to])


def pass2_odd(K, si, S, li):
    fw, ar = K.fw, K.ar
    ar.reset()
    nb = S // 128
    NR = K.NR
    SF = NR * S
    nkb = SF // 128
    kT = Ring(K, "kTh", 2, [128, SF], BF16)
    vH = Ring(K, "vh", 2, [128, nkb, 128], BF16)
    qT = Ring(K, "qT", 3, [128, 512], BF16)
    gT = Ring(K, "gT", 2, [128, 4, 128], BF16)
    PT = Ring(K, "PT", 4, [128, 512], BF16)
    rd = Ring(K, "rd", 2, [128, 512], F32)
    tmp = Ring(K, "tmp", 2, [128, 512], F32)
    yo = Ring(K, "yo", 3, [128, 4, 128], BF16)
    ps_s = PRing(K, [0, 1, 2])
    ps_o = PRing(K, [3, 4])
    ps_d = PRing(K, [5, 6])

    for kh in range(4):
        k_, tk = kT.next()
        v_, tv = vH.next()
        for r in range(NR):
            fw.dma("sp", k_[:, r * S:(r + 1) * S], K.kT_g[si][kh][r * 128:(r + 1) * 128, :], writes=[tk])
            for j in range(S // K.VS):
                kb0 = (r * S + j * K.VS) // 128
                fw.dma("sp", v_[:, kb0:kb0 + 8, :],
                       K.v_g[si][j][r * K.VS:(r + 1) * K.VS, kh * 128:(kh + 1) * 128].rearrange(
                           "(b p) c -> p b c", p=128), writes=[tv])
        for qb in range(nb):
            c0 = qb * 128
            q_, tq = qT.next()
            g_, tg = gT.next()
            fw.dma("sp", q_[:], K.s_qT[qb, kh], writes=[tq])
            fw.dma("sp", g_[:], K.s_fm[kh * 512:(kh + 1) * 512, c0:c0 + 128].rearrange("(g d) t -> d g t", d=128),
                   writes=[tg])
            po, tpo = ps_o.next()
            pd, tpd = ps_d.next()
            cur = None

            def qk(kb, k_=k_, tk=tk, q_=q_, tq=tq):
                p, tp = ps_s.next()
                fw.op("pe", lambda e, p=p, kb=kb: e.matmul(
                    p[:], lhsT=k_[:, kb * 128:(kb + 1) * 128], rhs=q_[:], start=True, stop=True),
                    reads=[tk, tq], writes=[tp])
                P_, tP = PT.next()
                fw.op("act", lambda e, P_=P_, p=p: e.activation(out=P_[:], in_=p[:], func=AF.Exp, scale=ATT_SCALE),
                      reads=[tp], writes=[tP])
                return P_, tP

            nxt = qk(0)
            for kb in range(nkb):
                cur = nxt
                if kb + 1 < nkb:
                    nxt = qk(kb + 1)
                P_, tP = cur
                fw.op("pe", lambda e, po=po, kb=kb, P_=P_, v_=v_: e.matmul(
                    po[:], lhsT=v_[:, kb, :], rhs=P_[:], start=(kb == 0), stop=(kb == nkb - 1)),
                    reads=[tv, tP], writes=[tpo])
                fw.op("pe", lambda e, pd=pd, kb=kb, P_=P_: e.matmul(
                    pd[:], lhsT=K.onesb[:], rhs=P_[:], start=(kb == 0), stop=(kb == nkb - 1)),
                    reads=[K.t_onesb, tP], writes=[tpd])
            r_, trd = rd.next()
            fw.op("dve", lambda e, r_=r_, pd=pd: e.reciprocal(out=r_[:], in_=pd[:]), reads=[tpd], writes=[trd])
            t_, tt = tmp.next()
            fw.op("dve", lambda e, t_=t_, po=po, r_=r_: e.tensor_tensor(out=t_[:], in0=po[:], in1=r_[:], op=ALU.mult),
                  reads=[tpo, trd], writes=[tt])
            o_, to = yo.next()
            fw.op("dve", lambda e, t_=t_, g_=g_, o_=o_: e.tensor_tensor(
                out=o_[:].rearrange("p g t -> p (g t)"), in0=t_[:], in1=g_[:].rearrange("p g t -> p (g t)"),
                op=ALU.mult), reads=[tt, tg], writes=[to])
            fw.dma("pool", K.s_cat[kh * 512:(kh + 1) * 512, c0:c0 + 128].rearrange("(g d) t -> d g t", d=128),
                   o_[:], reads=[to])


def pass3(K, si, S, wo_d, x_src):
    fw, ar = K.fw, K.ar
    ar.reset()
    nb = S // 128
    wo = ar.alloc("wo", [128, KC, D], BF16)
    t_wo = fw.tok("wo")
    for c in range(0, KC, 4):
        fw.dma("sp", wo[:, c:c + 4, :], wo_d[:, c:c + 4, :], writes=[t_wo])
    cT = Ring(K, "cT", 2, [128, KC, 128], BF16)
    xr = Ring(K, "xr", 2, [128, D], F32)
    ob = Ring(K, "o3", 2, [128, D], F32)
    ps = PRing(K, list(range(8)))
    y = K.y[si]
    for b in range(nb):
        c0 = b * 128
        c_, tc = cT.next()
        x_, tx = xr.next()
        o_, to = ob.next()
        fw.dma("sp", c_[:], K.s_cat[:, c0:c0 + 128].rearrange("(c p) t -> p c t", p=128), writes=[tc])
        fw.dma("sp", x_[:], x_src[c0:c0 + 128, :], writes=[tx])
        for nt in range(4):
            p, tp = ps.next()
            for c in range(KC):
                fw.op("pe", lambda e, p=p, c_=c_, c=c, nt=nt: e.matmul(
                    p[:], lhsT=c_[:, c, :], rhs=wo[:, c, nt * 512:(nt + 1) * 512], start=(c == 0), stop=(c == KC - 1)),
                    reads=[tc, t_wo], writes=[tp])
            fw.op("dve", lambda e, p=p, o_=o_, x_=x_, nt=nt: e.tensor_tensor(
                out=o_[:, nt * 512:(nt + 1) * 512], in0=p[:], in1=x_[:, nt * 512:(nt + 1) * 512], op=ALU.add),
                reads=[tp, tx], writes=[to])
        fw.dma("pool", y[c0:c0 + 128, :], o_[:], reads=[to])


def _rope_tables(S):
    pos = np.arange(S, dtype=np.float32)
    inv = np.power(np.float32(500000.0), -np.arange(16, dtype=np.float32) * np.float32(2.0 / 32)).astype(np.float32)
    ang = (pos[:, None] * inv[None, :]).astype(np.float32)
    c, s = np.cos(ang).astype(np.float32), np.sin(ang).astype(np.float32)
    ropeE = np.concatenate([c, c, -s, s], axis=1).astype(np.float32)
    row = (np.arange(S) // 64).astype(np.float32)
    col = (np.arange(S) % 64).astype(np.float32)
    inv2 = np.power(np.float32(10000.0), -np.arange(32, dtype=np.float32) * np.float32(2.0 / 64)).astype(np.float32)
    ar_ = (row[:, None] * inv2[None, :]).astype(np.float32)
    ac_ = (col[:, None] * inv2[None, :]).astype(np.float32)
    cr, sr, cc, sc = np.cos(ar_), np.sin(ar_), np.cos(ac_), np.sin(ac_)
    ropeO = np.concatenate([cr, cr, cc, cc, -sr, sr, -sc, sc], axis=1).astype(np.float32)
    return ropeE, ropeO


def _shared_inputs(inp):
    f = lambda a: np.ascontiguousarray(np.asarray(a, dtype=np.float32))
    rep = lambda v, n: np.ascontiguousarray(np.broadcast_to(np.tile(f(v), n)[None, :], (128, 128 * n)))
    m = {}
    nt = np.stack([f(inp["norm_ab"][0]), f(inp["norm_ab"][1]), f(inp["norm_c"][0]), f(inp["norm_c"][1])], 0)
    m["norm_t"] = np.ascontiguousarray(nt.reshape(4, 16, 128).transpose(2, 0, 1))
    m["w_in_ab"] = f(inp["w_in_ab"])
    m["w_out_ab"] = f(inp["w_out_ab"])
    m["w_in_c"] = f(inp["w_in_c"])
    m["w_out_c"] = f(inp["w_out_c"])
    m["wsT"] = np.ascontiguousarray(f(inp["a_w_s"]).transpose(0, 3, 1, 2))
    bs = f(inp["a_b_s"]).reshape(2, 1, 1024)
    m["bbc"] = np.ascontiguousarray(np.broadcast_to(bs, (2, 128, 1024)))
    m["vgcol"] = np.ascontiguousarray(f(inp["a_v_norm"]).reshape(2, 8, 128).transpose(0, 2, 1))
    m["sinkbc"] = np.ascontiguousarray(np.broadcast_to(f(inp["b_sink"]).reshape(2, 1, 8), (2, 128, 8)))
    m["gqE"] = np.stack([rep(inp["b_q_norm"][i], 4) for i in range(2)], 0)
    m["gkE"] = np.stack([rep(inp["b_k_norm"][i], 4) for i in range(2)], 0)
    m["gqO"] = np.stack([rep(inp["c_q_norm"][i], 4) for i in range(2)], 0)
    m["gkO"] = np.stack([rep(inp["c_k_norm"][i], 4) for i in range(2)], 0)
    m["ident"] = np.eye(128, dtype=np.float32)
    kk = np.arange(128)[:, None]
    qq = np.arange(128)[None, :]
    mprev = (qq <= kk).astype(np.float32)
    mnext = (kk <= qq).astype(np.float32)
    m["masks"] = np.ascontiguousarray(np.stack([np.tile(mprev, (1, 4)), np.tile(mnext, (1, 4))], 1))
    m["edge"] = np.zeros((128, 2, 512), dtype=np.float32)
    return m


def _edge_masks(masks, r):
    e = np.zeros_like(masks)
    if r == 1:
        e[:, 0, :] = masks[:, 0, :]
    if r == 0:
        e[:, 1, :] = masks[:, 1, :]
    return e


_CACHE = {}


def kernel(**inputs):
    xp = np.asarray(inputs["x_prompt"], dtype=np.float32)
    xs = np.asarray(inputs["x_sample"], dtype=np.float32)
    SP, SS = xp.shape[1], xs.shape[1]
    key = (SP, SS)
    if key not in _CACHE:
        _CACHE[key] = build_program([SP, SS], pair=True)[0]
    nc = _CACHE[key]
    shared = _shared_inputs(inputs)
    rE0, rO0 = _rope_tables(SP)
    rE1, rO1 = _rope_tables(SS)
    LP, LS = SP // 2, SS // 2
    in_maps = []
    for c in range(8):
        i, r = c // 2, c % 2
        m = dict(shared)
        m["x0"] = np.ascontiguousarray(xp[i, r * LP:(r + 1) * LP])
        m["x1"] = np.ascontiguousarray(xs[i, r * LS:(r + 1) * LS])
        m["ropeE0"] = np.ascontiguousarray(rE0[r * LP:(r + 1) * LP])
        m["ropeO0"] = np.ascontiguousarray(rO0[r * LP:(r + 1) * LP])
        m["ropeE1"] = np.ascontiguousarray(rE1[r * LS:(r + 1) * LS])
        m["ropeO1"] = np.ascontiguousarray(rO1[r * LS:(r + 1) * LS])
        m["edge"] = _edge_masks(shared["masks"], r)
        in_maps.append(m)
    res = run_bass_kernel_spmd(nc, in_maps, core_ids=list(range(8)))
    out = lambda c, k: np.asarray(res.results[c][k], dtype=np.float32)
    yp = np.stack([np.concatenate([out(2 * i, "y0"), out(2 * i + 1, "y0")], 0) for i in range(4)], 0)
    ys = np.stack([np.concatenate([out(2 * i, "y1"), out(2 * i + 1, "y1")], 0) for i in range(4)], 0)
    return (yp, ys)
```
